# Optimizing a Trainium2 kernel written in Bass

```python
import math
import jax, jax.numpy as jnp
from jax import lax
import numpy as np

D_MODEL = 1024
BATCH = 16
SEQ = 2048
DEPTH = 1

HEAD_DIM = 128
HEADS_PER_GROUP = 4
DILATED_GROUPS = ((128, 1), (512, 4), (2048, 16))
N_GROUPS = 3
N_ATTN_HEADS = N_GROUPS * HEADS_PER_GROUP
QKV_WIDTH = N_ATTN_HEADS * HEAD_DIM
ATTN_WIDTH = HEADS_PER_GROUP * HEAD_DIM
CONV_WIDTH = D_MODEL
CONV_K = 3
N_BUCKETS = 32
MAX_EXACT = 16
MAX_DISTANCE = 2048
BLOCK = 128
DEEPNORM_ALPHA = (2.0 * DEPTH) ** 0.25
DEEPNORM_BETA = (8.0 * DEPTH) ** -0.25
LN_EPS = 1e-5
NEG_INF = -1e30
COL_WIDTHS = (QKV_WIDTH, QKV_WIDTH, QKV_WIDTH, ATTN_WIDTH,
              CONV_WIDTH, CONV_WIDTH, CONV_WIDTH, CONV_WIDTH,
              D_MODEL, D_MODEL)
IN_COLS = 4 * QKV_WIDTH // 4 * 3 // 3 * 1 * 3 + ATTN_WIDTH + 4 * CONV_WIDTH + 2 * D_MODEL

kernel_name = "hybrid_dilated_attn_shortconv_gated_merge"


def _split_cols(proj):
    offsets = []
    acc = 0
    for w in COL_WIDTHS[:-1]:
        acc += w
        offsets.append(acc)
    return jnp.split(proj, offsets, axis=-1)


def _t5_bucket(dist):
    n = jnp.maximum(dist, 1).astype(jnp.float32)
    large = MAX_EXACT + (jnp.log(n / MAX_EXACT) / math.log(MAX_DISTANCE / MAX_EXACT)
                         * (N_BUCKETS - MAX_EXACT)).astype(jnp.int32)
    large = jnp.minimum(large, N_BUCKETS - 1)
    return jnp.where(dist < MAX_EXACT, dist, large)


def _dilated_group_attention(q, k, v, bias_tab, window, dilation):
    bsz, seq, n_h, e = q.shape
    sub_len = seq // dilation
    n_blk = -(-sub_len // BLOCK)
    sub_pad = n_blk * BLOCK
    n_steps = window // dilation

    def to_sub(t):
        t = t.reshape(bsz, sub_len, dilation, n_h, e).transpose(0, 2, 3, 1, 4)
        return jnp.pad(t, ((0, 0), (0, 0), (0, 0), (0, sub_pad - sub_len), (0, 0)))

    def band(t):
        t = jnp.pad(t, ((0, 0), (0, 0), (0, 0), (BLOCK, 0), (0, 0)))
        t = t.reshape(bsz, dilation, n_h, n_blk + 1, BLOCK, e)
        return jnp.concatenate([t[:, :, :, :-1], t[:, :, :, 1:]], axis=4)

    qb = to_sub(q).reshape(bsz, dilation, n_h, n_blk, BLOCK, e)
    kb = band(to_sub(k))
    vb = band(to_sub(v))

    a_idx = jnp.arange(BLOCK)[:, None]
    b_idx = jnp.arange(2 * BLOCK)[None, :]
    steps = a_idx + BLOCK - b_idx
    key_sub = jnp.arange(n_blk)[:, None, None] * BLOCK - BLOCK + b_idx[None]
    valid = ((steps >= 0) & (steps <= n_steps))[None] & (key_sub >= 0)
    bucket = _t5_bucket(jnp.maximum(steps, 0) * dilation)
    bias = bias_tab[bucket].transpose(2, 0, 1).astype(jnp.float32)

    s = jnp.einsum('bdhnqe,bdhnke->bdhnqk', qb, kb).astype(jnp.float32) * (HEAD_DIM ** -0.5)
    s = s + bias[None, None, :, None]
    s = jnp.where(valid[None, None, None], s, NEG_INF)
    lse = jax.nn.logsumexp(s, axis=-1)
    p = jnp.exp(s - lse[..., None]).astype(v.dtype)
    o = jnp.einsum('bdhnqk,bdhnke->bdhnqe', p, vb)
    o = o.reshape(bsz, dilation, n_h, sub_pad, e)[:, :, :, :sub_len]
    lse = lse.reshape(bsz, dilation, n_h, sub_pad)[..., :sub_len]
    o = o.transpose(0, 3, 1, 2, 4).reshape(bsz, seq, n_h, e)
    lse = lse.transpose(0, 3, 1, 2).reshape(bsz, seq, n_h)
    return o, lse


def _layer_norm(x, g, b):
    xf = x.astype(jnp.float32)
    mu = jnp.mean(xf, axis=-1, keepdims=True)
    var = jnp.mean(jnp.square(xf - mu), axis=-1, keepdims=True)
    return ((xf - mu) * lax.rsqrt(var + LN_EPS) * g + b).astype(x.dtype)


def setup_inputs(seed: int = 0) -> dict:
    key = jax.random.key(seed)
    ks = jax.random.split(key, 14)
    f32 = jnp.float32
    x = jax.random.normal(ks[0], (BATCH, SEQ, D_MODEL), f32)
    c = jax.random.normal(ks[1], (BATCH, D_MODEL), f32)
    w_ada = jax.random.normal(ks[2], (DEPTH, D_MODEL, 3 * D_MODEL), f32) * (0.1 * D_MODEL ** -0.5)
    b_ada = jax.random.normal(ks[3], (DEPTH, 3 * D_MODEL), f32) * 0.01
    n_cols = sum(COL_WIDTHS)
    col_scale = np.ones((n_cols,), np.float32)
    col_scale[2 * QKV_WIDTH:3 * QKV_WIDTH] = DEEPNORM_BETA
    w_in = jax.random.normal(ks[4], (DEPTH, D_MODEL, n_cols), f32) * (D_MODEL ** -0.5) * jnp.asarray(col_scale)
    conv_w = jax.random.normal(ks[5], (DEPTH, CONV_K, CONV_WIDTH), f32) * (CONV_K ** -0.5)
    conv_b = jax.random.normal(ks[6], (DEPTH, CONV_WIDTH), f32) * 0.01
    rel_bias = jax.random.normal(ks[7], (N_BUCKETS, N_ATTN_HEADS), f32) * 0.5
    w_attn_out = jax.random.normal(ks[8], (DEPTH, ATTN_WIDTH, D_MODEL), f32) * (ATTN_WIDTH ** -0.5) * DEEPNORM_BETA
    w_conv_out = jax.random.normal(ks[9], (DEPTH, CONV_WIDTH, D_MODEL), f32) * (CONV_WIDTH ** -0.5) * DEEPNORM_BETA
    w_o = jax.random.normal(ks[10], (DEPTH, D_MODEL, D_MODEL), f32) * (D_MODEL ** -0.5) * DEEPNORM_BETA
    ln_g = 1.0 + jax.random.normal(ks[11], (DEPTH, D_MODEL), f32) * 0.01
    ln_b = jax.random.normal(ks[12], (DEPTH, D_MODEL), f32) * 0.01
    return {"x": x, "c": c, "w_ada": w_ada, "b_ada": b_ada, "w_in": w_in,
            "conv_w": conv_w, "conv_b": conv_b, "rel_bias": rel_bias,
            "w_attn_out": w_attn_out, "w_conv_out": w_conv_out, "w_o": w_o,
            "ln_g": ln_g, "ln_b": ln_b}


def reference(x, c, w_ada, b_ada, w_in, conv_w, conv_b, rel_bias,
              w_attn_out, w_conv_out, w_o, ln_g, ln_b):
    bsz, seq, _ = x.shape
    for layer in range(DEPTH):
        mod = jax.nn.silu(c) @ w_ada[layer] + b_ada[layer]
        shift, scale, gate = jnp.split(mod, 3, axis=-1)
        h = x * (1.0 + scale[:, None]) + shift[:, None]

        proj = h @ w_in[layer]
        q, k, v, g_attn, u, b_gate, c_gate, g_conv, m_attn, m_conv = _split_cols(proj)

        q = q.reshape(bsz, seq, N_ATTN_HEADS, HEAD_DIM)
        k = k.reshape(bsz, seq, N_ATTN_HEADS, HEAD_DIM)
        v = v.reshape(bsz, seq, N_ATTN_HEADS, HEAD_DIM)
        outs, lses = [], []
        for gi, (window, dilation) in enumerate(DILATED_GROUPS):
            hs = slice(gi * HEADS_PER_GROUP, (gi + 1) * HEADS_PER_GROUP)
            o_g, lse_g = _dilated_group_attention(q[:, :, hs], k[:, :, hs], v[:, :, hs],
                                                  rel_bias[:, hs], window, dilation)
            outs.append(o_g)
            lses.append(lse_g)
        o_all = jnp.stack(outs, axis=0)
        wts = jax.nn.softmax(jnp.stack(lses, axis=0), axis=0)
        o = jnp.sum(wts[..., None].astype(o_all.dtype) * o_all, axis=0).reshape(bsz, seq, ATTN_WIDTH)
        a_out = (o * jax.nn.silu(g_attn)) @ w_attn_out[layer]

        z = c_gate * u
        zp = jnp.pad(z, ((0, 0), (CONV_K - 1, 0), (0, 0)))
        cw = conv_w[layer]
        y_conv = (cw[0] * zp[:, :-2] + cw[1] * zp[:, 1:-1] + cw[2] * zp[:, 2:]) + conv_b[layer]
        s_out = (b_gate * y_conv * jax.nn.silu(g_conv)) @ w_conv_out[layer]

        merged = jax.nn.sigmoid(m_attn) * a_out + jax.nn.sigmoid(m_conv) * s_out
        y = merged @ w_o[layer]

        x = _layer_norm(DEEPNORM_ALPHA * x + (1.0 + gate[:, None]) * y, ln_g[layer], ln_b[layer])
    return x
```

```python
import contextlib
import math
import os

import numpy as np

import concourse.bass as bass
import concourse.mybir as mybir
from concourse.bass_utils import run_bass_kernel_spmd

F32 = mybir.dt.float32
BF16 = mybir.dt.bfloat16
AF = mybir.ActivationFunctionType
ALU = mybir.AluOpType

N_CORES = 8
S = 2048
D = 1024
NSEQ = 2
KC = 8
N_COLS = 11264
ALPHA = 2.0 ** 0.25
LN_EPS = 1e-5
QK_SCALE = 128.0 ** -0.5
GROUPS = ((128, 1), (512, 4), (2048, 16))
ARENA_WORDS = 53000

ENGS = ("pe", "act", "dve", "pool", "sp")


class Prog:
    def __init__(self, nc):
        self.nc = nc
        self.ops = {e: [] for e in ENGS}
        self.state = {}
        self.dma_count = {}

    def _deps(self, eng, reads, writes):
        deps = set()
        for k in reads:
            st = self.state.get(k)
            if st is not None:
                deps.update(st[0])
        for k in writes:
            st = self.state.get(k)
            if st is not None:
                for w in st[0]:
                    if not (w[0] == "eng" and w[1] == eng and eng != "pool"):
                        deps.add(w)
                for r in st[1]:
                    if not (r[0] == "eng" and r[1] == eng and eng != "pool"):
                        deps.add(r)
        return deps

    def _commit(self, me, reads, writes):
        for k in reads:
            st = self.state.get(k)
            if st is None:
                self.state[k] = [[], [me]]
            else:
                st[1].append(me)
        for k in writes:
            self.state[k] = [[me], []]

    def op(self, eng, fn, reads=(), writes=()):
        deps = self._deps(eng, reads, writes)
        me = ("eng", eng, len(self.ops[eng]))
        self.ops[eng].append({"fn": fn, "deps": deps, "slot": None, "inc": False})
        self._commit(me, reads, writes)

    def dma(self, queue, fn, slot, reads=(), writes=()):
        deps = self._deps(None, reads, writes)
        cnt = self.dma_count.get(slot, 0) + 1
        self.dma_count[slot] = cnt
        if cnt > 1:
            deps.add(("dma", slot, 16 * (cnt - 1)))
        me = ("dma", slot, 16 * cnt)
        self.ops[queue].append({"fn": fn, "deps": deps, "slot": slot, "inc": False})
        self._commit(me, reads, writes)

    def alias(self, new_keys, old_keys):
        ws, rs = [], []
        for k in old_keys:
            st = self.state.get(k)
            if st is not None:
                ws.extend(st[0])
                rs.extend(st[1])
        ws = list(dict.fromkeys(ws))
        rs = list(dict.fromkeys(rs))
        for k in new_keys:
            self.state[k] = [list(ws), list(rs)]

    def emit(self):
        nc = self.nc
        for e in ENGS:
            for o in self.ops[e]:
                for d in o["deps"]:
                    if d[0] == "eng":
                        self.ops[d[1]][d[2]]["inc"] = True
        mile = {}
        for e in ENGS:
            c = 0
            for i, o in enumerate(self.ops[e]):
                if o["slot"] is None and o["inc"]:
                    c += 1
                    mile[(e, i)] = c
        slots = sorted(self.dma_count.keys())
        with contextlib.ExitStack() as es:
            esem = {e: es.enter_context(nc.semaphore("prog_" + e)) for e in ENGS}
            dsem = {s: es.enter_context(nc.semaphore("dma_" + s)) for s in slots}
            block = es.enter_context(nc.Block())

            def run(e, eng):
                seen = {}
                for o in self.ops[e]:
                    waits = {}
                    for d in o["deps"]:
                        if d[0] == "eng":
                            key = ("e", d[1])
                            val = mile[(d[1], d[2])]
                            sem = esem[d[1]]
                        else:
                            key = ("d", d[1])
                            val = d[2]
                            sem = dsem[d[1]]
                        if seen.get(key, 0) >= val:
                            continue
                        if key not in waits or waits[key][1] < val:
                            waits[key] = (sem, val)
                    for key, (sem, val) in waits.items():
                        eng.wait_ge(sem, val)
                        seen[key] = val
                    ins = o["fn"](eng)
                    if o["slot"] is not None:
                        ins.then_inc(dsem[o["slot"]], 16)
                    elif o["inc"]:
                        ins.then_inc(esem[e], 1)
                if e == "sp":
                    for s in slots:
                        eng.wait_ge(dsem[s], 16 * self.dma_count[s])

            @block.tensor
            def _(eng):
                run("pe", eng)

            @block.scalar
            def _(eng):
                run("act", eng)

            @block.vector
            def _(eng):
                run("dve", eng)

            @block.gpsimd
            def _(eng):
                run("pool", eng)

            @block.sync
            def _(eng):
                run("sp", eng)


class _Stop(Exception):
    pass


class Arena:
    def __init__(self, P, ap):
        self.P = P
        self.ap = ap
        self.bufs = []

    def alloc(self, off, nwords, keys, dt=F32):
        assert off >= 0 and off + nwords <= ARENA_WORDS, (off, nwords)
        old = []
        for (o2, n2, k2) in self.bufs:
            if o2 < off + nwords and off < o2 + n2:
                old.extend(k2)
        keys = list(keys)
        if old:
            self.P.alias(keys, old)
        self.bufs.append((off, nwords, keys))
        v = self.ap[:, off:off + nwords]
        return v if dt == F32 else v.bitcast(dt)


def _w_in_perm():
    perm = []
    for j in range(4):
        for g in range(3):
            h = g * 4 + j
            for sec in range(3):
                perm.extend(range(sec * 1536 + h * 128, sec * 1536 + (h + 1) * 128))
        perm.extend(range(4608 + j * 128, 4608 + (j + 1) * 128))
    for fc in range(8):
        for sec in range(4):
            perm.extend(range(5120 + sec * 1024 + fc * 128, 5120 + sec * 1024 + (fc + 1) * 128))
    for dc in range(8):
        for sec in range(2):
            perm.extend(range(9216 + sec * 1024 + dc * 128, 9216 + sec * 1024 + (dc + 1) * 128))
    perm = np.asarray(perm, dtype=np.int64)
    assert perm.shape[0] == N_COLS and np.unique(perm).shape[0] == N_COLS
    return perm


def _t5_bucket(dist):
    dist = np.asarray(dist, dtype=np.int32)
    n = np.maximum(dist, 1).astype(np.float32)
    large = 16 + (np.log(n / np.float32(16)) / np.float32(math.log(2048 / 16)) * np.float32(16)).astype(np.int32)
    large = np.minimum(large, 31)
    return np.where(dist < 16, dist, large)


def _bias_index_and_mask():
    b = np.arange(128)[:, None]
    a = np.arange(128)[None, :]
    steps_cur = a - b
    steps_prev = a + 128 - b
    mask = np.concatenate([(steps_cur >= 0), (steps_prev <= 128)], axis=1).astype(np.float32)
    idx = []
    for (_, dil) in GROUPS:
        cur = _t5_bucket(np.maximum(steps_cur, 0) * dil)
        prev = _t5_bucket(np.clip(steps_prev, 0, 128) * dil)
        idx.append(np.concatenate([cur, prev], axis=1))
    return np.stack(idx, 0), mask


def build_program(debug=None, stop=None):
    nc = bass.Bass("TRN2", target_bir_lowering=False)

    def din(name, shape):
        return nc.dram_tensor(name, list(shape), F32, kind="ExternalInput").ap()

    x_d = din("x", [NSEQ, S, D])
    cT_d = din("cT", [128, 16])
    wada_d = din("w_ada", [D, 3 * D])
    badaT_d = din("badaT", [128, 16])
    bgate_d = din("bgate", [128, D])
    win_d = din("w_in", [D, N_COLS])
    cwT_d = din("cwT", [128, 32])
    eb_d = din("ebsrc", [128, 12 * 256])
    mask_d = din("mask", [128, 256])
    ident_d = din("ident", [128, 128])
    wao_d = din("w_attn_out", [512, D])
    wco_d = din("w_conv_out", [D, D])
    wo_d = din("w_o", [D, D])
    lng_d = din("lng", [128, D])
    lnb_d = din("lnb", [128, D])
    y_d = nc.dram_tensor("y", [NSEQ, S, D], F32, kind="ExternalOutput").ap()

    win_v = win_d.rearrange("(k p) c -> p k c", p=128)
    wada_v = wada_d.rearrange("(k p) c -> p k c", p=128)

    arena_ap = nc.alloc_sbuf_tensor("arena", [128, ARENA_WORDS], F32).ap()
    ps = [nc.alloc_psum_tensor("psb%d" % i, [128, 512], F32).ap() for i in range(8)]

    P = Prog(nc)
    A = Arena(P, arena_ap)

    def pk(b):
        return [("ps", b, 0), ("ps", b, 1)]

    off = [0]

    def palloc(n, keys, dt=F32):
        v = A.alloc(off[0], n, keys, dt)
        off[0] += n
        return v

    ident = palloc(128, ["ident"])
    ones_bf = palloc(64, ["ones"], BF16)
    identb = palloc(64, ["identb"], BF16)
    EB = palloc(3072, ["EB"])
    maskm = palloc(256, ["mask"])
    cwT = palloc(32, ["cwT"])
    badaT = palloc(16, ["badaT"])
    scT = palloc(16, ["scT"])
    shT = palloc(16, ["shT"])
    cT = palloc(16, ["cT"])
    scb = palloc(8, ["scb"], BF16)
    epst = palloc(8, ["eps"])
    gate1 = palloc(2048, ["gate1"])
    bgate = palloc(1024, ["bgate"])
    lng = palloc(1024, ["lng"])
    lnb = palloc(1024, ["lnb"])
    wao = palloc(2048, ["wao"], BF16).rearrange("p (k c) -> p k c", k=4)
    wco = palloc(4096, ["wco"], BF16).rearrange("p (k c) -> p k c", k=8)
    wo = palloc(4096, ["wo"], BF16).rearrange("p (k c) -> p k c", k=8)
    HT_OFF = off[0]
    OG_OFF = HT_OFF + 8192
    R_OFF = OG_OFF + 4096
    R_WORDS = ARENA_WORDS - R_OFF
    assert R_WORDS >= 21504, R_WORDS

    sp_n = [0]

    def load(dst, src, key, queue="sp"):
        sp_n[0] += 1
        P.dma(queue, lambda e: e.dma_start(out=dst, in_=src), "c%d" % sp_n[0], writes=[key])

    load(ident, ident_d, "ident")
    load(cT, cT_d, "cT")
    load(badaT, badaT_d, "badaT")
    P.op("pool", lambda e: e.memset(ones_bf, 1.0), writes=["ones"])
    P.op("dve", lambda e: e.tensor_copy(out=identb, in_=ident), reads=["ident"], writes=["identb"])
    P.op("pool", lambda e: e.memset(epst, LN_EPS), writes=["eps"])

    wa = [A.alloc(R_OFF + i * 4096, 4096, [("wa", i)], BF16).rearrange("p (k c) -> p k c", k=8) for i in range(3)]
    Lb = [A.alloc(R_OFF + 12288 + i * 512, 512, [("Lb", i)], BF16).rearrange("p (k c) -> p k c", k=8) for i in range(2)]
    for i in range(3):
        P.dma("pool", lambda e, i=i: e.dma_start(out=wa[i], in_=wada_v[:, :, i * 1024:(i + 1) * 1024]),
              "wa%d" % i, writes=[("wa", i)])

    P.op("act", lambda e: e.activation(out=scb, in_=cT, func=AF.Silu), reads=["cT"], writes=["scb"])
    for b in range(NSEQ):
        for k in range(KC):
            P.op("dve", lambda e, b=b, k=k: e.tensor_copy(out=Lb[b][:, k, :],
                                                          in_=scb[:, 2 * k + b:2 * k + b + 1].to_broadcast([128, 128])),
                 reads=["scb"], writes=[("Lb", b)])
    for i, (dst, bank) in enumerate(((shT, 0), (scT, 1))):
        for j in range(KC):
            for k in range(KC):
                P.op("pe", lambda e, i=i, j=j, k=k, bank=bank: e.matmul(
                    ps[bank][:, 2 * j:2 * j + 2], lhsT=wa[i][:, k, j * 128:(j + 1) * 128], rhs=scb[:, 2 * k:2 * k + 2],
                    start=(k == 0), stop=(k == KC - 1)),
                    reads=[("wa", i), "scb"], writes=pk(bank))
        for b in range(NSEQ):
            if i == 0:
                P.op("dve", lambda e, b=b, bank=bank: e.tensor_tensor(
                    out=shT[:, b * 8:(b + 1) * 8], in0=ps[bank][:, b:16:2], in1=badaT[:, 0:8], op=ALU.add),
                    reads=pk(bank) + ["badaT"], writes=[("shT", b)])
            else:
                P.op("dve", lambda e, b=b, bank=bank: e.scalar_tensor_tensor(
                    out=scT[:, b * 8:(b + 1) * 8], in0=ps[bank][:, b:16:2], scalar=1.0, in1=badaT[:, 8:16],
                    op0=ALU.add, op1=ALU.add),
                    reads=pk(bank) + ["badaT"], writes=[("scT", b)])
    def late_consts():
        load(bgate, bgate_d, "bgate")
        load(EB, eb_d, "EB")
        load(maskm, mask_d, "mask")
        load(cwT, cwT_d, "cwT")
        load(lng, lng_d, "lng")
        load(lnb, lnb_d, "lnb")
        P.op("act", lambda e: e.activation(out=EB, in_=EB, func=AF.Exp), reads=["EB"], writes=["EB"])
        for h in range(12):
            P.op("dve", lambda e, h=h: e.tensor_tensor(out=EB[:, h * 256:(h + 1) * 256], in0=EB[:, h * 256:(h + 1) * 256],
                                                       in1=maskm, op=ALU.mult),
                 reads=["EB", "mask"], writes=[("EBm", h)])

    def compute_gate1():
        for b in range(NSEQ):
            for half in range(2):
                bank = 2 + b * 2 + half
                for k in range(KC):
                    P.op("pe", lambda e, b=b, half=half, k=k, bank=bank: e.matmul(
                        ps[bank][:, :], lhsT=Lb[b][:, k, :], rhs=wa[2][:, k, half * 512:(half + 1) * 512],
                        start=(k == 0), stop=(k == KC - 1)),
                        reads=[("Lb", b), ("wa", 2)], writes=pk(bank))
                P.op("dve", lambda e, b=b, half=half, bank=bank: e.scalar_tensor_tensor(
                    out=gate1[:, b * 1024 + half * 512: b * 1024 + (half + 1) * 512], in0=ps[bank][:, :], scalar=1.0,
                    in1=bgate[:, half * 512:(half + 1) * 512], op0=ALU.add, op1=ALU.add),
                    reads=pk(bank) + ["bgate"], writes=[("gate1", b)])


    bank_rr = [0]

    def next_bank(lo=0, n=8):
        b = lo + bank_rr[0] % n
        bank_rr[0] += 1
        return b

    dbg_outs = {}

    def dbg(name, ap, nwords_shape, reads):
        if debug is None or name not in debug:
            return
        t = nc.dram_tensor("dbg_" + name, list(nwords_shape), ap.dtype, kind="ExternalOutput").ap()
        dbg_outs[name] = t
        P.dma("sp", lambda e: e.dma_start(out=t, in_=ap), "dbg_" + name, reads=reads)

    def alloc_w1():
        WG = [A.alloc(R_OFF + g * 1536, 1536, [("WG", g)], BF16).rearrange("p (k c) -> p k c", k=8) for g in range(3)]
        Wg = A.alloc(R_OFF + 4608, 512, ["Wg"], BF16).rearrange("p (k c) -> p k c", k=8)
        return WG, Wg

    def load_wg(W1, j, g):
        cb = j * 1280 + g * 384
        buf = W1[0][g]
        P.dma("pool", lambda e: e.dma_start(out=buf, in_=win_v[:, :, cb:cb + 384]), "WG%d" % g, writes=[("WG", g)])

    def load_wgate(W1, j):
        cb = j * 1280 + 1152
        buf = W1[1]
        P.dma("pool", lambda e: e.dma_start(out=buf, in_=win_v[:, :, cb:cb + 128]), "Wg", writes=["Wg"])

    def load_wfc(Wbuf, fc):
        cbase = 5120 + fc * 512
        P.dma("pool", lambda e: e.dma_start(out=Wbuf, in_=win_v[:, :, cbase:cbase + 512]),
              "Wfc%d" % (fc % 2), writes=[("Wfc", fc % 2)])

    def load_wm(Wbuf, dc):
        cbase = 9216 + dc * 256
        P.dma("pool", lambda e: e.dma_start(out=Wbuf, in_=win_v[:, :, cbase:cbase + 256]),
              "Wm%d" % (dc % 2), writes=[("Wm", dc % 2)])

    HTK = [("hT", q, k) for q in range(4) for k in range(KC)]
    ACCK = [("acc", q) for q in range(4)]
    DENK = [("den", q) for q in range(4)]
    LAG = 4
    NE, NPT = 3, 6

    def drain(gen):
        for _ in gen:
            pass

    def interleave(ga, gb, nb, lead=0):
        a_done = b_done = False
        for _ in range(lead):
            try:
                next(gb)
            except StopIteration:
                b_done = True
                break
        while not (a_done and b_done):
            if not a_done:
                try:
                    next(ga)
                except StopIteration:
                    a_done = True
            for _ in range(nb if not a_done else 1000000):
                if b_done:
                    break
                try:
                    next(gb)
                except StopIteration:
                    b_done = True

    def chain(*gens):
        for g_ in gens:
            if g_ is not None:
                yield from g_

    def stage0_gen(b, split_evac):
        hT = A.alloc(HT_OFF, 8192, HTK, BF16).rearrange("p (k t) -> p k t", k=8)
        xs = [A.alloc(OG_OFF + i * 1024, 1024, [("xs", i)]) for i in range(4)]

        def xs_load(t):
            xb = xs[t % 4]
            P.dma("sp", lambda e: e.dma_start(out=xb, in_=x_d[b, t * 128:(t + 1) * 128, :]),
                  "xs%d" % (t % 4), writes=[("xs", t % 4)])
        xs_load(0)
        xs_load(1)
        xs_load(2)
        for t in range(16):
            if t + 3 < 16:
                xs_load(t + 3)
            xb = xs[t % 4]
            for kq in range(2):
                bank = next_bank()
                for k4 in range(4):
                    k = kq * 4 + k4
                    P.op("pe", lambda e, xb=xb, k=k, k4=k4, bank=bank: e.transpose(
                        ps[bank][:, k4 * 128:(k4 + 1) * 128], xb[:, k * 128:(k + 1) * 128], ident),
                        reads=[("xs", t % 4), "ident"], writes=pk(bank))
                for k4 in range(4):
                    k = kq * 4 + k4
                    if split_evac and kq == 1:
                        P.op("dve", lambda e, k=k, k4=k4, bank=bank, t=t: e.tensor_scalar(
                            out=hT[:, k, t * 128:(t + 1) * 128], in0=ps[bank][:, k4 * 128:(k4 + 1) * 128],
                            scalar1=scT[:, b * 8 + k:b * 8 + k + 1], scalar2=shT[:, b * 8 + k:b * 8 + k + 1],
                            op0=ALU.mult, op1=ALU.add),
                            reads=pk(bank) + [("scT", b), ("shT", b)], writes=[("hT", t // 4, k)])
                    else:
                        P.op("act", lambda e, k=k, k4=k4, bank=bank, t=t: e.activation(
                            out=hT[:, k, t * 128:(t + 1) * 128], in_=ps[bank][:, k4 * 128:(k4 + 1) * 128], func=AF.Identity,
                            scale=scT[:, b * 8 + k:b * 8 + k + 1], bias=shT[:, b * 8 + k:b * 8 + k + 1]),
                            reads=pk(bank) + [("scT", b), ("shT", b)], writes=[("hT", t // 4, k)])
            yield

    def seq_body(b, W1, has_next):
        hT = arena_ap[:, HT_OFF:HT_OFF + 8192].bitcast(BF16).rearrange("p (k t) -> p k t", k=8)
        og = A.alloc(OG_OFF, 4096, [("og", j) for j in range(4)], BF16).rearrange("p (j t) -> p j t", j=4)
        WG, Wg = W1
        VTKEYS = [("VT", q) for q in range(4)]
        VT = A.alloc(OG_OFF + 3072, 1024, VTKEYS, BF16)
        wstg = arena_ap[:, OG_OFF + 3072:OG_OFF + 4096]
        wo_v = wo_d.rearrange("(k p) c -> p k c", p=128)
        for k in range(KC):
            P.dma("sp", lambda e, k=k: e.dma_start(out=wstg, in_=wo_v[:, k, :]), "wstg", writes=VTKEYS)
            P.op("pool", lambda e, k=k: e.tensor_tensor(out=wo[:, k, :], in0=wstg, in1=gate1[:, b * 1024:(b + 1) * 1024], op=ALU.mult),
                 reads=VTKEYS + [("gate1", b)], writes=["wo"])
        ro = R_OFF + 5120
        QT = A.alloc(ro, 3072, [("QT", g, q) for g in range(3) for q in range(4)], BF16).rearrange("p (g t) -> p g t", g=3); ro += 3072
        KT = A.alloc(ro, 3072, [("KT", g, q) for g in range(3) for q in range(4)], BF16).rearrange("p (g t) -> p g t", g=3); ro += 3072
        V = A.alloc(ro, 3072, [("V", g, q) for g in range(3) for q in range(4)], BF16).rearrange("p (g t) -> p g t", g=3); ro += 3072
        gs = A.alloc(ro, 1024, [("gs", q) for q in range(4)], BF16); ro += 1024
        acc = A.alloc(ro, 2048, ACCK); ro += 2048
        den = A.alloc(ro, 2048, DENK); ro += 2048
        Eb = []
        for i in range(NE):
            Eb.append(A.alloc(ro, 256, [("E", i)])); ro += 256
        PT = []
        for i in range(NPT):
            PT.append(A.alloc(ro, 128, [("PT", i)], BF16)); ro += 128
        assert ro <= ARENA_WORDS

        def proj_gen(j, g):
            Wb = WG[g]
            wk = ("WG", g)
            for q in range(4):
                bank = next_bank(0, 2)
                for k in range(KC):
                    P.op("pe", lambda e, q=q, k=k, bank=bank: e.matmul(
                        ps[bank][:, :], lhsT=Wb[:, k, 0:128], rhs=hT[:, k, q * 512:(q + 1) * 512],
                        start=(k == 0), stop=(k == KC - 1)), reads=[wk, ("hT", q, k)], writes=pk(bank))
                P.op("act", lambda e, q=q, bank=bank: e.activation(out=QT[:, g, q * 512:(q + 1) * 512], in_=ps[bank][:, :],
                                                                   func=AF.Identity),
                     reads=pk(bank), writes=[("QT", g, q)])
                yield
            for q in range(4):
                bank = next_bank(0, 2)
                for k in range(KC):
                    P.op("pe", lambda e, q=q, k=k, bank=bank: e.matmul(
                        ps[bank][:, :], lhsT=Wb[:, k, 128:256], rhs=hT[:, k, q * 512:(q + 1) * 512],
                        start=(k == 0), stop=(k == KC - 1)), reads=[wk, ("hT", q, k)], writes=pk(bank))
                P.op("act", lambda e, q=q, bank=bank: e.activation(out=KT[:, g, q * 512:(q + 1) * 512], in_=ps[bank][:, :],
                                                                   func=AF.Identity),
                     reads=pk(bank), writes=[("KT", g, q)])
                yield
            dil = GROUPS[g][1]
            nblk = 16 // dil
            for q in range(4):
                bank = next_bank(0, 2)
                for k in range(KC):
                    P.op("pe", lambda e, q=q, k=k, bank=bank: e.matmul(
                        ps[bank][:, :], lhsT=Wb[:, k, 256:384], rhs=hT[:, k, q * 512:(q + 1) * 512],
                        start=(k == 0), stop=(k == KC - 1)), reads=[wk, ("hT", q, k)], writes=pk(bank))
                P.op("act", lambda e, q=q, bank=bank: e.activation(out=VT[:, q * 512:(q + 1) * 512], in_=ps[bank][:, :],
                                                                   func=AF.Identity),
                     reads=pk(bank), writes=[("VT", q)])
                yield
            VTg = VT.rearrange("p (l r) -> p r l", r=dil)
            VTK = [("VT", q) for q in range(4)]
            for b4 in range(4):
                bank = next_bank(0, 2)
                pb = ps[bank].bitcast(BF16)
                for i4 in range(4):
                    blk = b4 * 4 + i4
                    r, n = blk // nblk, blk % nblk
                    P.op("pe", lambda e, r=r, n=n, i4=i4, pb=pb: e.transpose(
                        pb[:, i4 * 128:(i4 + 1) * 128], VTg[:, r, n * 128:(n + 1) * 128], identb),
                        reads=VTK + ["identb"], writes=pk(bank))
                P.op("dve", lambda e, b4=b4, pb=pb: e.tensor_copy(out=V[:, g, b4 * 512:(b4 + 1) * 512], in_=pb[:, 0:512]),
                     reads=pk(bank), writes=[("V", g, b4)])
                yield
            if j < 3:
                load_wg(W1, j + 1, g)

        def gproj_gen(j):
            for q in range(4):
                bank = next_bank(0, 2)
                for k in range(KC):
                    P.op("pe", lambda e, q=q, k=k, bank=bank: e.matmul(
                        ps[bank][:, :], lhsT=Wg[:, k, :], rhs=hT[:, k, q * 512:(q + 1) * 512],
                        start=(k == 0), stop=(k == KC - 1)), reads=["Wg", ("hT", q, k)], writes=pk(bank))
                P.op("act", lambda e, q=q, bank=bank: e.activation(out=gs[:, q * 512:(q + 1) * 512], in_=ps[bank][:, :], func=AF.Silu),
                     reads=pk(bank), writes=[("gs", q)])
                yield
            if j < 3:
                load_wgate(W1, j + 1)

        def attn_gen(j, g):
            dil = GROUPS[g][1]
            nblk = 16 // dil
            h = g * 4 + j
            QTg = QT[:, g, :].rearrange("p (l r) -> p r l", r=dil)
            KTg = KT[:, g, :].rearrange("p (l r) -> p r l", r=dil)
            QK_KEYS = [("QT", g, q) for q in range(4)] + [("KT", g, q) for q in range(4)]
            if g == 1:
                P.alias([("acc1", x) for x in range(8)], ACCK)
                P.alias([("den1", x) for x in range(8)], DENK)
            elif g == 2:
                P.alias([("acc2", x) for x in range(8)], [("acc1", x) for x in range(8)])
                P.alias([("den2", x) for x in range(8)], [("den1", x) for x in range(8)])

            def qk_step(i):
                r, m = i // nblk, i % nblk
                nq = 256 if m < nblk - 1 else 128
                sb = 2 + i % 4
                P.op("pe", lambda e: e.matmul(
                    ps[sb][:, 0:nq], lhsT=KTg[:, r, m * 128:(m + 1) * 128],
                    rhs=QTg[:, r, m * 128:m * 128 + nq], start=True, stop=True),
                    reads=QK_KEYS, writes=pk(sb))
                E = Eb[i % NE]
                P.op("act", lambda e: e.activation(out=E[:, 0:nq], in_=ps[sb][:, 0:nq], func=AF.Exp, scale=QK_SCALE),
                     reads=pk(sb), writes=[("E", i % NE)])
                pt = PT[i % NPT]
                P.op("pool", lambda e: e.tensor_tensor(out=pt[:, 0:nq], in0=E[:, 0:nq], in1=EB[:, h * 256:h * 256 + nq],
                                                      op=ALU.mult),
                     reads=[("E", i % NE), ("EBm", h)], writes=[("PT", i % NPT)])

            def pv_step(i):
                r, m = i // nblk, i % nblk
                if g == 0:
                    bq, slot = m // 2, m % 2
                elif g == 1:
                    bq, slot = r * 2 + m // 2, m % 2
                else:
                    bq, slot = r // 2, r % 2
                bank = 6 + bq % 2
                vkeys = [("V", g, q) for q in range(4)]
                vprev = V[:, g, (i - 1) * 128:i * 128] if m >= 1 else None
                vcur = V[:, g, i * 128:(i + 1) * 128]
                for (c0, lhs_prev, lhs_cur, rk) in ((0, vprev, vcur, vkeys), (256, ones_bf, ones_bf, ["ones"])):
                    cs = slice(c0 + slot * 128, c0 + (slot + 1) * 128)
                    if m >= 1:
                        ptp = PT[(i - 1) % NPT]
                        P.op("pe", lambda e, cs=cs, lhs_prev=lhs_prev, ptp=ptp: e.matmul(
                            ps[bank][:, cs], lhsT=lhs_prev, rhs=ptp[:, 128:256], start=True, stop=False),
                            reads=rk + [("PT", (i - 1) % NPT)], writes=pk(bank))
                    ptc = PT[i % NPT]
                    P.op("pe", lambda e, cs=cs, lhs_cur=lhs_cur, ptc=ptc: e.matmul(
                        ps[bank][:, cs], lhsT=lhs_cur, rhs=ptc[:, 0:128], start=(m == 0), stop=True),
                        reads=rk + [("PT", i % NPT)], writes=pk(bank))
                if slot == 1:
                    for (c0, dstbuf, dkeys, gname) in ((0, acc, ACCK, "acc"), (256, den, DENK, "den")):
                        if g == 0:
                            dv = dstbuf[:, bq * 256:(bq + 1) * 256]
                            sv = ps[bank][:, c0:c0 + 256]
                            kk = [dkeys[bq // 2]]
                        elif g == 1:
                            dv = dstbuf.rearrange("p (l r) -> p r l", r=4)[:, bq // 2, (bq % 2) * 256:(bq % 2 + 1) * 256]
                            sv = ps[bank][:, c0:c0 + 256]
                            kk = [(gname + "1", bq)]
                        else:
                            dv = dstbuf.rearrange("p (l r) -> p r l", r=16)[:, bq * 2:(bq + 1) * 2, :]
                            sv = ps[bank][:, c0:c0 + 256].rearrange("p (r l) -> p r l", r=2)
                            kk = [(gname + "2", bq)]
                        if g == 0:
                            P.op("dve", lambda e, dv=dv, sv=sv: e.tensor_copy(out=dv, in_=sv), reads=pk(bank), writes=kk)
                        else:
                            P.op("dve", lambda e, dv=dv, sv=sv: e.tensor_tensor(out=dv, in0=sv, in1=dv, op=ALU.add),
                                 reads=pk(bank) + kk, writes=kk)

            for i in range(16 + LAG):
                if i < 16:
                    qk_step(i)
                if i >= LAG:
                    pv_step(i - LAG)
                yield

        def merge_slot(j):
            if j == 3:
                P.alias([("og", 3)], [("og", 3)] + VTKEYS)
            P.alias(ACCK, [("acc2", x) for x in range(8)])
            P.alias(DENK, [("den2", x) for x in range(8)])
            for q in range(4):
                sl = slice(q * 512, (q + 1) * 512)
                P.op("dve", lambda e, sl=sl: e.reciprocal(out=den[:, sl], in_=den[:, sl]), reads=[("den", q)], writes=[("den", q)])
                P.op("dve", lambda e, sl=sl: e.tensor_tensor(out=acc[:, sl], in0=acc[:, sl], in1=den[:, sl], op=ALU.mult),
                     reads=[("acc", q), ("den", q)], writes=[("acc", q)])
                P.op("pool", lambda e, sl=sl, j=j: e.tensor_tensor(out=og[:, j, sl], in0=acc[:, sl], in1=gs[:, sl], op=ALU.mult),
                     reads=[("acc", q), ("gs", q)], writes=[("og", j)])

        drain(proj_gen(0, 0))
        drain(proj_gen(0, 1))
        if b == 0:
            P.dma("pool", lambda e: e.dma_start(out=wao, in_=wao_d.rearrange("(k p) c -> p k c", p=128)), "wao", writes=["wao"])
            P.dma("pool", lambda e: e.dma_start(out=wco, in_=wco_d.rearrange("(k p) c -> p k c", p=128)), "wco", writes=["wco"])
        Wfc = None
        for j in range(4):
            interleave(attn_gen(j, 0), chain(proj_gen(j, 2), gproj_gen(j)), 2, lead=8)
            interleave(attn_gen(j, 1), proj_gen(j + 1, 0) if j < 3 else iter(()), 2)
            if j == 3:
                Wfc = [A.alloc(R_OFF + i * 2048, 2048, [("Wfc", i)], BF16).rearrange("p (k c) -> p k c", k=8) for i in range(2)]
                load_wfc(Wfc[0], 0)
                load_wfc(Wfc[1], 1)
            interleave(attn_gen(j, 2), proj_gen(j + 1, 1) if j < 3 else iter(()), 2)
            merge_slot(j)
        if stop == "s1":
            raise _Stop()

        ro = R_OFF + 5120
        sg = A.alloc(ro, 8192, [("sg", fc) for fc in range(8)], BF16).rearrange("p (k t) -> p k t", k=8); ro += 8192
        zb = []
        for i in range(2):
            zb.append(A.alloc(ro, 2056, [("z", i)])); ro += 2056
        tmp3 = []
        for i in range(2):
            d_ = {}
            for nm in ("u", "y", "sgl"):
                d_[nm] = A.alloc(ro, 512, [(nm, i)]); ro += 512
            tmp3.append(d_)
        assert ro <= ARENA_WORDS
        for i in range(2):
            P.op("pool", lambda e, i=i: e.memset(zb[i][:, 0:2], 0.0), writes=[("z", i)])
        it3 = 0
        Wm = None
        for fc in range(8):
            W = Wfc[fc % 2]
            z = zb[fc % 2]
            zk = ("z", fc % 2)
            for q in range(4):
                T = tmp3[it3 % 2]
                ti = it3 % 2
                it3 += 1
                banks = [(it3 % 2) * 4 + s_ for s_ in range(4)]
                for s_ in (0, 3, 2, 1):
                    for k in range(KC):
                        P.op("pe", lambda e, s_=s_, k=k, W=W, q=q, bank=banks[s_]: e.matmul(
                            ps[bank][:, :], lhsT=W[:, k, s_ * 128:(s_ + 1) * 128], rhs=hT[:, k, q * 512:(q + 1) * 512],
                            start=(k == 0), stop=(k == KC - 1)), reads=[("Wfc", fc % 2), ("hT", q, k)], writes=pk(banks[s_]))
                bu, bb, bc, bg = banks
                zs = slice(2 + q * 512, 2 + (q + 1) * 512)
                P.op("act", lambda e, T=T, bu=bu: e.activation(out=T["u"], in_=ps[bu][:, :], func=AF.Identity),
                     reads=pk(bu), writes=[("u", ti)])
                P.op("act", lambda e, T=T, bg=bg: e.activation(out=T["sgl"], in_=ps[bg][:, :], func=AF.Silu),
                     reads=pk(bg), writes=[("sgl", ti)])
                P.op("dve", lambda e, T=T, bc=bc, z=z, zs=zs: e.tensor_tensor(out=z[:, zs], in0=ps[bc][:, :], in1=T["u"], op=ALU.mult),
                     reads=pk(bc) + [("u", ti)], writes=[zk])
                P.op("dve", lambda e, T=T, bb=bb: e.tensor_tensor(out=T["sgl"], in0=ps[bb][:, :], in1=T["sgl"], op=ALU.mult),
                     reads=pk(bb) + [("sgl", ti)], writes=[("sgl", ti)])
                P.op("act", lambda e, T=T, z=z, zs=zs, fc=fc: e.activation(
                    out=T["y"], in_=z[:, zs], func=AF.Identity, scale=cwT[:, fc * 4 + 2:fc * 4 + 3], bias=cwT[:, fc * 4 + 3:fc * 4 + 4]),
                    reads=[zk, "cwT"], writes=[("y", ti)])
                P.op("dve", lambda e, T=T, z=z, q=q, fc=fc: e.scalar_tensor_tensor(
                    out=T["y"], in0=z[:, 1 + q * 512:1 + (q + 1) * 512], scalar=cwT[:, fc * 4 + 1:fc * 4 + 2], in1=T["y"],
                    op0=ALU.mult, op1=ALU.add), reads=[zk, "cwT", ("y", ti)], writes=[("y", ti)])
                P.op("dve", lambda e, T=T, z=z, q=q, fc=fc: e.scalar_tensor_tensor(
                    out=T["y"], in0=z[:, q * 512:(q + 1) * 512], scalar=cwT[:, fc * 4 + 0:fc * 4 + 1], in1=T["y"],
                    op0=ALU.mult, op1=ALU.add), reads=[zk, "cwT", ("y", ti)], writes=[("y", ti)])
                P.op("pool", lambda e, T=T, fc=fc, q=q: e.tensor_tensor(out=sg[:, fc, q * 512:(q + 1) * 512], in0=T["y"], in1=T["sgl"],
                                                                        op=ALU.mult),
                     reads=[("y", ti), ("sgl", ti)], writes=[("sg", fc)])
            if fc + 2 < 8:
                load_wfc(W, fc + 2)
            if fc == 6:
                Wm = [A.alloc(R_OFF + i * 1024, 1024, [("Wm", i)], BF16).rearrange("p (k c) -> p k c", k=8) for i in range(2)]
                load_wm(Wm[0], 0)
                load_wm(Wm[1], 1)
        if stop == "s3":
            raise _Stop()

        merged = A.alloc(R_OFF + 13312, 8192, [("mg", q) for q in range(4)], BF16).rearrange("p (k t) -> p k t", k=8)
        ro = R_OFF + 2048
        tmp4 = []
        for i in range(2):
            d_ = {}
            for nm in ("sa", "sc"):
                d_[nm] = A.alloc(ro, 512, [(nm, i)]); ro += 512
            tmp4.append(d_)
        assert ro <= R_OFF + 5120
        SGK = [("sg", fc) for fc in range(8)]
        OGK = [("og", j) for j in range(4)]
        it4 = 0
        W1n = None
        for dc in range(8):
            W = Wm[dc % 2]
            for q in range(4):
                T = tmp4[it4 % 2]
                ti = it4 % 2
                it4 += 1
                bA, bMA, bS, bMC = [(it4 % 2) * 4 + s_ for s_ in range(4)]
                qs = slice(q * 512, (q + 1) * 512)
                for k in range(KC):
                    P.op("pe", lambda e, k=k, W=W, qs=qs, bMA=bMA: e.matmul(
                        ps[bMA][:, :], lhsT=W[:, k, 0:128], rhs=hT[:, k, qs], start=(k == 0), stop=(k == KC - 1)),
                        reads=[("Wm", dc % 2), ("hT", q, k)], writes=pk(bMA))
                for k in range(KC):
                    P.op("pe", lambda e, k=k, W=W, qs=qs, bMC=bMC: e.matmul(
                        ps[bMC][:, :], lhsT=W[:, k, 128:256], rhs=hT[:, k, qs], start=(k == 0), stop=(k == KC - 1)),
                        reads=[("Wm", dc % 2), ("hT", q, k)], writes=pk(bMC))
                for k in range(4):
                    P.op("pe", lambda e, k=k, qs=qs, bA=bA, dc=dc: e.matmul(
                        ps[bA][:, :], lhsT=wao[:, k, dc * 128:(dc + 1) * 128], rhs=og[:, k, qs], start=(k == 0), stop=(k == 3)),
                        reads=["wao"] + OGK, writes=pk(bA))
                for k in range(KC):
                    P.op("pe", lambda e, k=k, qs=qs, bS=bS, dc=dc: e.matmul(
                        ps[bS][:, :], lhsT=wco[:, k, dc * 128:(dc + 1) * 128], rhs=sg[:, k, qs], start=(k == 0), stop=(k == KC - 1)),
                        reads=["wco"] + SGK, writes=pk(bS))
                P.op("act", lambda e, T=T, bMA=bMA: e.activation(out=T["sa"], in_=ps[bMA][:, :], func=AF.Sigmoid),
                     reads=pk(bMA), writes=[("sa", ti)])
                P.op("act", lambda e, T=T, bMC=bMC: e.activation(out=T["sc"], in_=ps[bMC][:, :], func=AF.Sigmoid),
                     reads=pk(bMC), writes=[("sc", ti)])
                P.op("dve", lambda e, T=T, bA=bA: e.tensor_tensor(out=T["sa"], in0=ps[bA][:, :], in1=T["sa"], op=ALU.mult),
                     reads=pk(bA) + [("sa", ti)], writes=[("sa", ti)])
                P.op("dve", lambda e, T=T, bS=bS: e.tensor_tensor(out=T["sc"], in0=ps[bS][:, :], in1=T["sc"], op=ALU.mult),
                     reads=pk(bS) + [("sc", ti)], writes=[("sc", ti)])
                P.op("pool", lambda e, T=T, dc=dc, qs=qs: e.tensor_tensor(out=merged[:, dc, qs], in0=T["sa"], in1=T["sc"], op=ALU.add),
                     reads=[("sa", ti), ("sc", ti)], writes=[("mg", q)])
            if dc + 2 < 8:
                load_wm(W, dc + 2)
        if stop == "s4":
            raise _Stop()
        s0n = None
        if has_next:
            W1n = alloc_w1()
            for g_ in range(3):
                load_wg(W1n, 0, g_)
            load_wgate(W1n, 0)
            s0n = stage0_gen(b + 1, False)

        ro = R_OFF + 5120
        xr, rb, obuf = [], [], []
        for i in range(3):
            xr.append(A.alloc(ro, 1024, [("xr", i)])); ro += 1024
        for i in range(4):
            rb.append(A.alloc(ro, 1024, [("rb", i)])); ro += 1024
        obuf = rb
        stt = [A.alloc(ro + i * 16, 16, [("st", i)]) for i in range(4)]
        ro += 64
        mvt = [A.alloc(ro + i * 8, 8, [("mv", i)]) for i in range(4)]
        ro += 32
        assert ro <= R_OFF + 13312

        def xr_load(t):
            P.dma("sp", lambda e: e.dma_start(out=xr[t % 3], in_=x_d[b, t * 128:(t + 1) * 128, :]),
                  "xr%d" % (t % 3), writes=[("xr", t % 3)])
        xr_load(0)
        xr_load(1)
        for t in range(16):
            i2 = t % 4
            i3 = t % 3
            if t + 2 < 16:
                xr_load(t + 2)
            banks = [(t % 4) * 2, (t % 4) * 2 + 1]
            for half in range(2):
                for k in range(KC):
                    P.op("pe", lambda e, k=k, half=half, t=t, bank=banks[half]: e.matmul(
                        ps[bank][:, :], lhsT=merged[:, k, t * 128:(t + 1) * 128], rhs=wo[:, k, half * 512:(half + 1) * 512],
                        start=(k == 0), stop=(k == KC - 1)), reads=["wo", ("mg", t // 4)], writes=pk(banks[half]))
            for half in range(2):
                hs = slice(half * 512, (half + 1) * 512)
                P.op("dve", lambda e, half=half, hs=hs, i2=i2, i3=i3, bank=banks[half]: e.scalar_tensor_tensor(
                    out=rb[i2][:, hs], in0=xr[i3][:, hs], scalar=ALPHA, in1=ps[bank][:, :], op0=ALU.mult, op1=ALU.add),
                    reads=pk(banks[half]) + [("xr", i3)], writes=[("rb", i2)])
            for c_ in range(2):
                P.op("dve", lambda e, i2=i2, c_=c_: e.bn_stats(out=stt[i2][:, c_ * 6:(c_ + 1) * 6], in_=rb[i2][:, c_ * 512:(c_ + 1) * 512]),
                     reads=[("rb", i2)], writes=[("st", i2)])
            mv = mvt[i2]
            P.op("dve", lambda e, i2=i2, mv=mv: e.bn_aggr(out=mv[:, 0:2], in_=stt[i2][:, 0:12]), reads=[("st", i2)], writes=[("mv", i2)])
            P.op("act", lambda e, mv=mv: e.activation(out=mv[:, 2:3], in_=mv[:, 1:2], func=AF.Sqrt, bias=epst[:, 0:1], scale=1.0),
                 reads=[("mv", i2), "eps"], writes=[("mv", i2)])
            P.op("dve", lambda e, mv=mv: e.reciprocal(out=mv[:, 3:4], in_=mv[:, 2:3]), reads=[("mv", i2)], writes=[("mv", i2)])
            P.op("dve", lambda e, mv=mv: e.scalar_tensor_tensor(out=mv[:, 4:5], in0=mv[:, 0:1], scalar=-1.0, in1=mv[:, 3:4],
                                                                op0=ALU.mult, op1=ALU.mult),
                 reads=[("mv", i2)], writes=[("mv", i2)])
            P.op("act", lambda e, mv=mv, i2=i2: e.activation(out=obuf[i2], in_=rb[i2], func=AF.Identity, scale=mv[:, 3:4], bias=mv[:, 4:5]),
                 reads=[("rb", i2), ("mv", i2)], writes=[("rb", i2)])
            P.op("dve", lambda e, i2=i2: e.tensor_tensor(out=obuf[i2], in0=obuf[i2], in1=lng, op=ALU.mult),
                 reads=[("rb", i2), "lng"], writes=[("rb", i2)])
            P.op("pool", lambda e, i2=i2: e.tensor_tensor(out=obuf[i2], in0=obuf[i2], in1=lnb, op=ALU.add),
                 reads=[("rb", i2), "lnb"], writes=[("rb", i2)])
            P.dma("pool", lambda e, t=t, i2=i2: e.dma_start(out=y_d[b, t * 128:(t + 1) * 128, :], in_=obuf[i2]),
                  "yo%d" % i2, reads=[("rb", i2)])
            if s0n is not None:
                next(s0n, None)
        if s0n is not None:
            drain(s0n)
        return W1n

    W1c = alloc_w1()
    for g_ in range(3):
        load_wg(W1c, 0, g_)
    load_wgate(W1c, 0)
    try:
        if stop != "pro":
            drain(stage0_gen(0, True))
            late_consts()
            compute_gate1()
            if stop == "s0":
                raise _Stop()
            nseq = NSEQ if stop is None else 1
            for b_ in range(nseq):
                W1c = seq_body(b_, W1c, b_ + 1 < nseq)
    except _Stop:
        pass

    P.emit()
    return nc, dbg_outs


def prep_inputs(x, c, w_ada, b_ada, w_in, conv_w, conv_b, rel_bias, w_attn_out, w_conv_out, w_o, ln_g, ln_b):
    f = lambda a: np.ascontiguousarray(np.asarray(a, dtype=np.float32))
    x = f(x); c = f(c)
    perm = _w_in_perm()
    w_in_p = f(np.asarray(w_in)[0][:, perm])
    b_ada0 = np.asarray(b_ada, dtype=np.float32)[0]
    badaT = f(b_ada0[:2048].reshape(16, 128).T)
    bgate = f(np.broadcast_to(b_ada0[2048:][None, :], (128, D)))
    cw = np.asarray(conv_w, dtype=np.float32)[0]
    cbv = np.asarray(conv_b, dtype=np.float32)[0]
    cw4 = np.concatenate([cw, cbv[None, :]], axis=0)
    cwT = f(cw4.reshape(4, 8, 128).transpose(2, 1, 0).reshape(128, 32))
    idx, mask = _bias_index_and_mask()
    rb = np.asarray(rel_bias, dtype=np.float32)
    eb = np.empty((128, 12, 256), np.float32)
    for g in range(3):
        for j in range(4):
            h = g * 4 + j
            eb[:, h, :] = rb[:, h][idx[g]]
    shared = {
        "w_ada": f(np.asarray(w_ada)[0]), "badaT": badaT, "bgate": bgate, "w_in": w_in_p, "cwT": cwT,
        "ebsrc": f(eb.reshape(128, 12 * 256)), "mask": f(mask), "ident": np.eye(128, dtype=np.float32),
        "w_attn_out": f(np.asarray(w_attn_out)[0]), "w_conv_out": f(np.asarray(w_conv_out)[0]), "w_o": f(np.asarray(w_o)[0]),
        "lng": f(np.broadcast_to(np.asarray(ln_g, dtype=np.float32)[0][None, :], (128, D))),
        "lnb": f(np.broadcast_to(np.asarray(ln_b, dtype=np.float32)[0][None, :], (128, D))),
    }
    in_maps = []
    for i in range(N_CORES):
        m = dict(shared)
        m["x"] = f(x[i * NSEQ:(i + 1) * NSEQ])
        cc = c[i * NSEQ:(i + 1) * NSEQ]
        m["cT"] = f(cc.reshape(NSEQ, 8, 128).transpose(2, 1, 0).reshape(128, 16))
        in_maps.append(m)
    return in_maps


_NC_CACHE = {}


def kernel(x, c, w_ada, b_ada, w_in, conv_w, conv_b, rel_bias, w_attn_out, w_conv_out, w_o, ln_g, ln_b):
    in_maps = prep_inputs(x, c, w_ada, b_ada, w_in, conv_w, conv_b, rel_bias, w_attn_out, w_conv_out, w_o, ln_g, ln_b)
    if "nc" not in _NC_CACHE:
        _NC_CACHE["nc"] = build_program()[0]
    nc = _NC_CACHE["nc"]
    res = run_bass_kernel_spmd(nc, in_maps, core_ids=list(range(N_CORES)))
    out = np.concatenate([np.asarray(r["y"], dtype=np.float32) for r in res.results], axis=0)
    return out.reshape(N_CORES * NSEQ, S, D)
```

```python
import contextlib
import math
import os

import numpy as np

import concourse.bass as bass
import concourse.mybir as mybir
from concourse.bass_utils import run_bass_kernel_spmd

F32 = mybir.dt.float32
BF16 = mybir.dt.bfloat16
AF = mybir.ActivationFunctionType
ALU = mybir.AluOpType

N_CORES = 8
S = 2048
D = 1024
NSEQ = 2
KC = 8
N_COLS = 11264
ALPHA = 2.0 ** 0.25
LN_EPS = 1e-5
QK_SCALE = 128.0 ** -0.5
GROUPS = ((128, 1), (512, 4), (2048, 16))
ARENA_WORDS = 53000

ENGS = ("pe", "act", "dve", "pool", "sp")


class Prog:
    def __init__(self, nc):
        self.nc = nc
        self.ops = {e: [] for e in ENGS}
        self.state = {}
        self.dma_count = {}

    def _deps(self, eng, reads, writes):
        deps = set()
        for k in reads:
            st = self.state.get(k)
            if st is not None:
                deps.update(st[0])
        for k in writes:
            st = self.state.get(k)
            if st is not None:
                for w in st[0]:
                    if not (w[0] == "eng" and w[1] == eng and eng != "pool"):
                        deps.add(w)
                for r in st[1]:
                    if not (r[0] == "eng" and r[1] == eng and eng != "pool"):
                        deps.add(r)
        return deps

    def _commit(self, me, reads, writes):
        for k in reads:
            st = self.state.get(k)
            if st is None:
                self.state[k] = [[], [me]]
            else:
                st[1].append(me)
        for k in writes:
            self.state[k] = [[me], []]

    def op(self, eng, fn, reads=(), writes=()):
        deps = self._deps(eng, reads, writes)
        me = ("eng", eng, len(self.ops[eng]))
        self.ops[eng].append({"fn": fn, "deps": deps, "slot": None, "inc": False})
        self._commit(me, reads, writes)

    def dma(self, queue, fn, slot, reads=(), writes=()):
        deps = self._deps(None, reads, writes)
        cnt = self.dma_count.get(slot, 0) + 1
        self.dma_count[slot] = cnt
        if cnt > 1:
            deps.add(("dma", slot, 16 * (cnt - 1)))
        me = ("dma", slot, 16 * cnt)
        self.ops[queue].append({"fn": fn, "deps": deps, "slot": slot, "inc": False})
        self._commit(me, reads, writes)

    def alias(self, new_keys, old_keys):
        ws, rs = [], []
        for k in old_keys:
            st = self.state.get(k)
            if st is not None:
                ws.extend(st[0])
                rs.extend(st[1])
        ws = list(dict.fromkeys(ws))
        rs = list(dict.fromkeys(rs))
        for k in new_keys:
            self.state[k] = [list(ws), list(rs)]

    def emit(self):
        nc = self.nc
        for e in ENGS:
            for o in self.ops[e]:
                for d in o["deps"]:
                    if d[0] == "eng":
                        self.ops[d[1]][d[2]]["inc"] = True
        mile = {}
        for e in ENGS:
            c = 0
            for i, o in enumerate(self.ops[e]):
                if o["slot"] is None and o["inc"]:
                    c += 1
                    mile[(e, i)] = c
        slots = sorted(self.dma_count.keys())
        with contextlib.ExitStack() as es:
            esem = {e: es.enter_context(nc.semaphore("prog_" + e)) for e in ENGS}
            dsem = {s: es.enter_context(nc.semaphore("dma_" + s)) for s in slots}
            block = es.enter_context(nc.Block())

            def run(e, eng):
                seen = {}
                for o in self.ops[e]:
                    waits = {}
                    for d in o["deps"]:
                        if d[0] == "eng":
                            key = ("e", d[1])
                            val = mile[(d[1], d[2])]
                            sem = esem[d[1]]
                        else:
                            key = ("d", d[1])
                            val = d[2]
                            sem = dsem[d[1]]
                        if seen.get(key, 0) >= val:
                            continue
                        if key not in waits or waits[key][1] < val:
                            waits[key] = (sem, val)
                    for key, (sem, val) in waits.items():
                        eng.wait_ge(sem, val)
                        seen[key] = val
                    ins = o["fn"](eng)
                    if o["slot"] is not None:
                        ins.then_inc(dsem[o["slot"]], 16)
                    elif o["inc"]:
                        ins.then_inc(esem[e], 1)
                if e == "sp":
                    for s in slots:
                        eng.wait_ge(dsem[s], 16 * self.dma_count[s])

            @block.tensor
            def _(eng):
                run("pe", eng)

            @block.scalar
            def _(eng):
                run("act", eng)

            @block.vector
            def _(eng):
                run("dve", eng)

            @block.gpsimd
            def _(eng):
                run("pool", eng)

            @block.sync
            def _(eng):
                run("sp", eng)


class _Stop(Exception):
    pass


class Arena:
    def __init__(self, P, ap):
        self.P = P
        self.ap = ap
        self.bufs = []

    def alloc(self, off, nwords, keys, dt=F32):
        assert off >= 0 and off + nwords <= ARENA_WORDS, (off, nwords)
        old = []
        for (o2, n2, k2) in self.bufs:
            if o2 < off + nwords and off < o2 + n2:
                old.extend(k2)
        keys = list(keys)
        if old:
            self.P.alias(keys, old)
        self.bufs.append((off, nwords, keys))
        v = self.ap[:, off:off + nwords]
        return v if dt == F32 else v.bitcast(dt)


def _w_in_perm():
    perm = []
    for j in range(4):
        for g in range(3):
            h = g * 4 + j
            for sec in range(3):
                perm.extend(range(sec * 1536 + h * 128, sec * 1536 + (h + 1) * 128))
        perm.extend(range(4608 + j * 128, 4608 + (j + 1) * 128))
    for fc in range(8):
        for sec in range(4):
            perm.extend(range(5120 + sec * 1024 + fc * 128, 5120 + sec * 1024 + (fc + 1) * 128))
    for dc in range(8):
        for sec in range(2):
            perm.extend(range(9216 + sec * 1024 + dc * 128, 9216 + sec * 1024 + (dc + 1) * 128))
    perm = np.asarray(perm, dtype=np.int64)
    assert perm.shape[0] == N_COLS and np.unique(perm).shape[0] == N_COLS
    return perm


def _t5_bucket(dist):
    dist = np.asarray(dist, dtype=np.int32)
    n = np.maximum(dist, 1).astype(np.float32)
    large = 16 + (np.log(n / np.float32(16)) / np.float32(math.log(2048 / 16)) * np.float32(16)).astype(np.int32)
    large = np.minimum(large, 31)
    return np.where(dist < 16, dist, large)


def _bias_index_and_mask():
    b = np.arange(128)[:, None]
    a = np.arange(128)[None, :]
    steps_cur = a - b
    steps_prev = a + 128 - b
    mask = np.concatenate([(steps_cur >= 0), (steps_prev <= 128)], axis=1).astype(np.float32)
    idx = []
    for (_, dil) in GROUPS:
        cur = _t5_bucket(np.maximum(steps_cur, 0) * dil)
        prev = _t5_bucket(np.clip(steps_prev, 0, 128) * dil)
        idx.append(np.concatenate([cur, prev], axis=1))
    return np.stack(idx, 0), mask


def build_program(debug=None, stop=None):
    nc = bass.Bass("TRN2", target_bir_lowering=False)

    def din(name, shape):
        return nc.dram_tensor(name, list(shape), F32, kind="ExternalInput").ap()

    x_d = din("x", [NSEQ, S, D])
    cT_d = din("cT", [128, 16])
    wada_d = din("w_ada", [D, 3 * D])
    badaT_d = din("badaT", [128, 16])
    bgate_d = din("bgate", [128, D])
    win_d = din("w_in", [D, N_COLS])
    cwT_d = din("cwT", [128, 32])
    eb_d = din("ebsrc", [128, 12 * 256])
    mask_d = din("mask", [128, 256])
    ident_d = din("ident", [128, 128])
    wao_d = din("w_attn_out", [512, D])
    wco_d = din("w_conv_out", [D, D])
    wo_d = din("w_o", [D, D])
    lng_d = din("lng", [128, D])
    lnb_d = din("lnb", [128, D])
    y_d = nc.dram_tensor("y", [NSEQ, S, D], F32, kind="ExternalOutput").ap()

    win_v = win_d.rearrange("(k p) c -> p k c", p=128)
    wada_v = wada_d.rearrange("(k p) c -> p k c", p=128)

    arena_ap = nc.alloc_sbuf_tensor("arena", [128, ARENA_WORDS], F32).ap()
    ps = [nc.alloc_psum_tensor("psb%d" % i, [128, 512], F32).ap() for i in range(8)]

    P = Prog(nc)
    A = Arena(P, arena_ap)

    def pk(b):
        return [("ps", b, 0), ("ps", b, 1)]

    off = [0]

    def palloc(n, keys, dt=F32):
        v = A.alloc(off[0], n, keys, dt)
        off[0] += n
        return v

    ident = palloc(128, ["ident"])
    ones_bf = palloc(64, ["ones"], BF16)
    EB = palloc(3072, ["EB"])
    maskm = palloc(256, ["mask"])
    cwT = palloc(32, ["cwT"])
    badaT = palloc(16, ["badaT"])
    scT = palloc(16, ["scT"])
    shT = palloc(16, ["shT"])
    cT = palloc(16, ["cT"])
    scb = palloc(8, ["scb"], BF16)
    epst = palloc(8, ["eps"])
    gate1 = palloc(2048, ["gate1"])
    bgate = palloc(1024, ["bgate"])
    lng = palloc(1024, ["lng"])
    lnb = palloc(1024, ["lnb"])
    wao = palloc(2048, ["wao"], BF16).rearrange("p (k c) -> p k c", k=4)
    wco = palloc(4096, ["wco"], BF16).rearrange("p (k c) -> p k c", k=8)
    wo = palloc(4096, ["wo"], BF16).rearrange("p (k c) -> p k c", k=8)
    HT_OFF = off[0]
    OG_OFF = HT_OFF + 8192
    R_OFF = OG_OFF + 4096
    R_WORDS = ARENA_WORDS - R_OFF
    assert R_WORDS >= 21504, R_WORDS

    sp_n = [0]

    def load(dst, src, key, queue="sp"):
        sp_n[0] += 1
        P.dma(queue, lambda e: e.dma_start(out=dst, in_=src), "c%d" % sp_n[0], writes=[key])

    load(ident, ident_d, "ident")
    load(cT, cT_d, "cT")
    load(badaT, badaT_d, "badaT")
    P.op("pool", lambda e: e.memset(ones_bf, 1.0), writes=["ones"])
    P.op("pool", lambda e: e.memset(epst, LN_EPS), writes=["eps"])

    wa = [A.alloc(R_OFF + i * 4096, 4096, [("wa", i)], BF16).rearrange("p (k c) -> p k c", k=8) for i in range(3)]
    Lb = [A.alloc(R_OFF + 12288 + i * 512, 512, [("Lb", i)], BF16).rearrange("p (k c) -> p k c", k=8) for i in range(2)]
    for i in range(3):
        P.dma("pool", lambda e, i=i: e.dma_start(out=wa[i], in_=wada_v[:, :, i * 1024:(i + 1) * 1024]),
              "wa%d" % i, writes=[("wa", i)])

    P.op("act", lambda e: e.activation(out=scb, in_=cT, func=AF.Silu), reads=["cT"], writes=["scb"])
    for b in range(NSEQ):
        for k in range(KC):
            P.op("dve", lambda e, b=b, k=k: e.tensor_copy(out=Lb[b][:, k, :],
                                                          in_=scb[:, 2 * k + b:2 * k + b + 1].to_broadcast([128, 128])),
                 reads=["scb"], writes=[("Lb", b)])
    for i, (dst, bank) in enumerate(((shT, 0), (scT, 1))):
        for j in range(KC):
            for k in range(KC):
                P.op("pe", lambda e, i=i, j=j, k=k, bank=bank: e.matmul(
                    ps[bank][:, 2 * j:2 * j + 2], lhsT=wa[i][:, k, j * 128:(j + 1) * 128], rhs=scb[:, 2 * k:2 * k + 2],
                    start=(k == 0), stop=(k == KC - 1)),
                    reads=[("wa", i), "scb"], writes=pk(bank))
        for b in range(NSEQ):
            if i == 0:
                P.op("dve", lambda e, b=b, bank=bank: e.tensor_tensor(
                    out=shT[:, b * 8:(b + 1) * 8], in0=ps[bank][:, b:16:2], in1=badaT[:, 0:8], op=ALU.add),
                    reads=pk(bank) + ["badaT"], writes=[("shT", b)])
            else:
                P.op("dve", lambda e, b=b, bank=bank: e.scalar_tensor_tensor(
                    out=scT[:, b * 8:(b + 1) * 8], in0=ps[bank][:, b:16:2], scalar=1.0, in1=badaT[:, 8:16],
                    op0=ALU.add, op1=ALU.add),
                    reads=pk(bank) + ["badaT"], writes=[("scT", b)])
    def late_consts():
        load(bgate, bgate_d, "bgate")
        load(EB, eb_d, "EB")
        load(maskm, mask_d, "mask")
        load(cwT, cwT_d, "cwT")
        load(lng, lng_d, "lng")
        load(lnb, lnb_d, "lnb")
        P.op("act", lambda e: e.activation(out=EB, in_=EB, func=AF.Exp), reads=["EB"], writes=["EB"])
        for h in range(12):
            P.op("dve", lambda e, h=h: e.tensor_tensor(out=EB[:, h * 256:(h + 1) * 256], in0=EB[:, h * 256:(h + 1) * 256],
                                                       in1=maskm, op=ALU.mult),
                 reads=["EB", "mask"], writes=[("EBm", h)])

    def compute_gate1():
        for b in range(NSEQ):
            for half in range(2):
                bank = 2 + b * 2 + half
                for k in range(KC):
                    P.op("pe", lambda e, b=b, half=half, k=k, bank=bank: e.matmul(
                        ps[bank][:, :], lhsT=Lb[b][:, k, :], rhs=wa[2][:, k, half * 512:(half + 1) * 512],
                        start=(k == 0), stop=(k == KC - 1)),
                        reads=[("Lb", b), ("wa", 2)], writes=pk(bank))
                P.op("dve", lambda e, b=b, half=half, bank=bank: e.scalar_tensor_tensor(
                    out=gate1[:, b * 1024 + half * 512: b * 1024 + (half + 1) * 512], in0=ps[bank][:, :], scalar=1.0,
                    in1=bgate[:, half * 512:(half + 1) * 512], op0=ALU.add, op1=ALU.add),
                    reads=pk(bank) + ["bgate"], writes=[("gate1", b)])


    bank_rr = [0]

    def next_bank(lo=0, n=8):
        b = lo + bank_rr[0] % n
        bank_rr[0] += 1
        return b

    dbg_outs = {}

    def dbg(name, ap, nwords_shape, reads):
        if debug is None or name not in debug:
            return
        t = nc.dram_tensor("dbg_" + name, list(nwords_shape), ap.dtype, kind="ExternalOutput").ap()
        dbg_outs[name] = t
        P.dma("sp", lambda e: e.dma_start(out=t, in_=ap), "dbg_" + name, reads=reads)

    def alloc_w1():
        WG = [A.alloc(R_OFF + g * 1536, 1536, [("WG", g)], BF16).rearrange("p (k c) -> p k c", k=8) for g in range(3)]
        Wg = A.alloc(R_OFF + 4608, 512, ["Wg"], BF16).rearrange("p (k c) -> p k c", k=8)
        return WG, Wg

    def load_wg(W1, j, g):
        cb = j * 1280 + g * 384
        buf = W1[0][g]
        P.dma("pool", lambda e: e.dma_start(out=buf, in_=win_v[:, :, cb:cb + 384]), "WG%d" % g, writes=[("WG", g)])

    def load_wgate(W1, j):
        cb = j * 1280 + 1152
        buf = W1[1]
        P.dma("pool", lambda e: e.dma_start(out=buf, in_=win_v[:, :, cb:cb + 128]), "Wg", writes=["Wg"])

    def load_wfc(Wbuf, fc):
        cbase = 5120 + fc * 512
        P.dma("pool", lambda e: e.dma_start(out=Wbuf, in_=win_v[:, :, cbase:cbase + 512]),
              "Wfc%d" % (fc % 2), writes=[("Wfc", fc % 2)])

    def load_wm(Wbuf, dc):
        cbase = 9216 + dc * 256
        P.dma("pool", lambda e: e.dma_start(out=Wbuf, in_=win_v[:, :, cbase:cbase + 256]),
              "Wm%d" % (dc % 2), writes=[("Wm", dc % 2)])

    HTK = [("hT", q, k) for q in range(4) for k in range(KC)]
    ACCK = [("acc", q) for q in range(4)]
    DENK = [("den", q) for q in range(4)]
    LAG = 4
    NE, NPT = 3, 6

    def drain(gen):
        for _ in gen:
            pass

    def interleave(ga, gb, nb, lead=0):
        a_done = b_done = False
        for _ in range(lead):
            try:
                next(gb)
            except StopIteration:
                b_done = True
                break
        while not (a_done and b_done):
            if not a_done:
                try:
                    next(ga)
                except StopIteration:
                    a_done = True
            for _ in range(nb if not a_done else 1000000):
                if b_done:
                    break
                try:
                    next(gb)
                except StopIteration:
                    b_done = True

    def chain(*gens):
        for g_ in gens:
            if g_ is not None:
                yield from g_

    def stage0_gen(b, split_evac):
        hT = A.alloc(HT_OFF, 8192, HTK, BF16).rearrange("p (k t) -> p k t", k=8)
        xs = [A.alloc(OG_OFF + i * 1024, 1024, [("xs", i)]) for i in range(4)]

        def xs_load(t):
            xb = xs[t % 4]
            P.dma("sp", lambda e: e.dma_start(out=xb, in_=x_d[b, t * 128:(t + 1) * 128, :]),
                  "xs%d" % (t % 4), writes=[("xs", t % 4)])
        xs_load(0)
        xs_load(1)
        xs_load(2)
        for t in range(16):
            if t + 3 < 16:
                xs_load(t + 3)
            xb = xs[t % 4]
            for kq in range(2):
                bank = next_bank()
                for k4 in range(4):
                    k = kq * 4 + k4
                    P.op("pe", lambda e, xb=xb, k=k, k4=k4, bank=bank: e.transpose(
                        ps[bank][:, k4 * 128:(k4 + 1) * 128], xb[:, k * 128:(k + 1) * 128], ident),
                        reads=[("xs", t % 4), "ident"], writes=pk(bank))
                for k4 in range(4):
                    k = kq * 4 + k4
                    if split_evac and kq == 1:
                        P.op("dve", lambda e, k=k, k4=k4, bank=bank, t=t: e.tensor_scalar(
                            out=hT[:, k, t * 128:(t + 1) * 128], in0=ps[bank][:, k4 * 128:(k4 + 1) * 128],
                            scalar1=scT[:, b * 8 + k:b * 8 + k + 1], scalar2=shT[:, b * 8 + k:b * 8 + k + 1],
                            op0=ALU.mult, op1=ALU.add),
                            reads=pk(bank) + [("scT", b), ("shT", b)], writes=[("hT", t // 4, k)])
                    else:
                        P.op("act", lambda e, k=k, k4=k4, bank=bank, t=t: e.activation(
                            out=hT[:, k, t * 128:(t + 1) * 128], in_=ps[bank][:, k4 * 128:(k4 + 1) * 128], func=AF.Identity,
                            scale=scT[:, b * 8 + k:b * 8 + k + 1], bias=shT[:, b * 8 + k:b * 8 + k + 1]),
                            reads=pk(bank) + [("scT", b), ("shT", b)], writes=[("hT", t // 4, k)])
            yield

    def seq_body(b, W1, has_next):
        hT = arena_ap[:, HT_OFF:HT_OFF + 8192].bitcast(BF16).rearrange("p (k t) -> p k t", k=8)
        og = A.alloc(OG_OFF, 4096, [("og", j) for j in range(4)], BF16).rearrange("p (j t) -> p j t", j=4)
        WG, Wg = W1
        wstg = arena_ap[:, OG_OFF + 3072:OG_OFF + 4096]
        wo_v = wo_d.rearrange("(k p) c -> p k c", p=128)
        for k in range(KC):
            P.dma("sp", lambda e, k=k: e.dma_start(out=wstg, in_=wo_v[:, k, :]), "wstg", writes=[("og", 3)])
            P.op("pool", lambda e, k=k: e.tensor_tensor(out=wo[:, k, :], in0=wstg, in1=gate1[:, b * 1024:(b + 1) * 1024], op=ALU.mult),
                 reads=[("og", 3), ("gate1", b)], writes=["wo"])
        ro = R_OFF + 5120
        QT = A.alloc(ro, 3072, [("QT", g, q) for g in range(3) for q in range(4)], BF16).rearrange("p (g t) -> p g t", g=3); ro += 3072
        KT = A.alloc(ro, 3072, [("KT", g, q) for g in range(3) for q in range(4)], BF16).rearrange("p (g t) -> p g t", g=3); ro += 3072
        V = A.alloc(ro, 3072, [("V", g, q) for g in range(3) for q in range(4)], BF16).rearrange("p (g t) -> p g t", g=3); ro += 3072
        gs = A.alloc(ro, 1024, [("gs", q) for q in range(4)], BF16); ro += 1024
        acc = A.alloc(ro, 2048, ACCK); ro += 2048
        den = A.alloc(ro, 2048, DENK); ro += 2048
        Eb = []
        for i in range(NE):
            Eb.append(A.alloc(ro, 256, [("E", i)])); ro += 256
        PT = []
        for i in range(NPT):
            PT.append(A.alloc(ro, 128, [("PT", i)], BF16)); ro += 128
        assert ro <= ARENA_WORDS

        def proj_gen(j, g):
            Wb = WG[g]
            wk = ("WG", g)
            for q in range(4):
                bank = next_bank(0, 2)
                for k in range(KC):
                    P.op("pe", lambda e, q=q, k=k, bank=bank: e.matmul(
                        ps[bank][:, :], lhsT=Wb[:, k, 0:128], rhs=hT[:, k, q * 512:(q + 1) * 512],
                        start=(k == 0), stop=(k == KC - 1)), reads=[wk, ("hT", q, k)], writes=pk(bank))
                P.op("act", lambda e, q=q, bank=bank: e.activation(out=QT[:, g, q * 512:(q + 1) * 512], in_=ps[bank][:, :],
                                                                   func=AF.Identity),
                     reads=pk(bank), writes=[("QT", g, q)])
                yield
            for q in range(4):
                bank = next_bank(0, 2)
                for k in range(KC):
                    P.op("pe", lambda e, q=q, k=k, bank=bank: e.matmul(
                        ps[bank][:, :], lhsT=Wb[:, k, 128:256], rhs=hT[:, k, q * 512:(q + 1) * 512],
                        start=(k == 0), stop=(k == KC - 1)), reads=[wk, ("hT", q, k)], writes=pk(bank))
                P.op("act", lambda e, q=q, bank=bank: e.activation(out=KT[:, g, q * 512:(q + 1) * 512], in_=ps[bank][:, :],
                                                                   func=AF.Identity),
                     reads=pk(bank), writes=[("KT", g, q)])
                yield
            dil = GROUPS[g][1]
            nblk = 16 // dil
            for b4 in range(4):
                bank = next_bank(0, 2)
                for i4 in range(4):
                    blk = b4 * 4 + i4
                    r, n = blk // nblk, blk % nblk
                    for k in range(KC):
                        hsub = hT[:, k, :].rearrange("p (l r) -> p r l", r=dil)
                        P.op("pe", lambda e, k=k, hsub=hsub, r=r, n=n, i4=i4, bank=bank: e.matmul(
                            ps[bank][:, i4 * 128:(i4 + 1) * 128], lhsT=hsub[:, r, n * 128:(n + 1) * 128],
                            rhs=Wb[:, k, 256:384], start=(k == 0), stop=(k == KC - 1)),
                            reads=[wk] + [("hT", q_, k) for q_ in range(4)], writes=pk(bank))
                    if i4 < 3:
                        yield
                if False:
                    pass
                else:
                    P.op("act", lambda e, b4=b4, bank=bank: e.activation(out=V[:, g, b4 * 512:(b4 + 1) * 512], in_=ps[bank][:, :],
                                                                         func=AF.Identity),
                         reads=pk(bank), writes=[("V", g, b4)])
                yield
            if j < 3:
                load_wg(W1, j + 1, g)

        def gproj_gen(j):
            for q in range(4):
                bank = next_bank(0, 2)
                for k in range(KC):
                    P.op("pe", lambda e, q=q, k=k, bank=bank: e.matmul(
                        ps[bank][:, :], lhsT=Wg[:, k, :], rhs=hT[:, k, q * 512:(q + 1) * 512],
                        start=(k == 0), stop=(k == KC - 1)), reads=["Wg", ("hT", q, k)], writes=pk(bank))
                P.op("act", lambda e, q=q, bank=bank: e.activation(out=gs[:, q * 512:(q + 1) * 512], in_=ps[bank][:, :], func=AF.Silu),
                     reads=pk(bank), writes=[("gs", q)])
                yield
            if j < 3:
                load_wgate(W1, j + 1)

        def attn_gen(j, g):
            dil = GROUPS[g][1]
            nblk = 16 // dil
            h = g * 4 + j
            QTg = QT[:, g, :].rearrange("p (l r) -> p r l", r=dil)
            KTg = KT[:, g, :].rearrange("p (l r) -> p r l", r=dil)
            QK_KEYS = [("QT", g, q) for q in range(4)] + [("KT", g, q) for q in range(4)]
            if g == 1:
                P.alias([("acc1", x) for x in range(8)], ACCK)
                P.alias([("den1", x) for x in range(8)], DENK)
            elif g == 2:
                P.alias([("acc2", x) for x in range(8)], [("acc1", x) for x in range(8)])
                P.alias([("den2", x) for x in range(8)], [("den1", x) for x in range(8)])

            def qk_step(i):
                r, m = i // nblk, i % nblk
                nq = 256 if m < nblk - 1 else 128
                sb = 2 + i % 4
                P.op("pe", lambda e: e.matmul(
                    ps[sb][:, 0:nq], lhsT=KTg[:, r, m * 128:(m + 1) * 128],
                    rhs=QTg[:, r, m * 128:m * 128 + nq], start=True, stop=True),
                    reads=QK_KEYS, writes=pk(sb))
                E = Eb[i % NE]
                P.op("act", lambda e: e.activation(out=E[:, 0:nq], in_=ps[sb][:, 0:nq], func=AF.Exp, scale=QK_SCALE),
                     reads=pk(sb), writes=[("E", i % NE)])
                pt = PT[i % NPT]
                P.op("pool", lambda e: e.tensor_tensor(out=pt[:, 0:nq], in0=E[:, 0:nq], in1=EB[:, h * 256:h * 256 + nq],
                                                      op=ALU.mult),
                     reads=[("E", i % NE), ("EBm", h)], writes=[("PT", i % NPT)])

            def pv_step(i):
                r, m = i // nblk, i % nblk
                if g == 0:
                    bq, slot = m // 2, m % 2
                elif g == 1:
                    bq, slot = r * 2 + m // 2, m % 2
                else:
                    bq, slot = r // 2, r % 2
                bank = 6 + bq % 2
                vkeys = [("V", g, q) for q in range(4)]
                vprev = V[:, g, (i - 1) * 128:i * 128] if m >= 1 else None
                vcur = V[:, g, i * 128:(i + 1) * 128]
                for (c0, lhs_prev, lhs_cur, rk) in ((0, vprev, vcur, vkeys), (256, ones_bf, ones_bf, ["ones"])):
                    cs = slice(c0 + slot * 128, c0 + (slot + 1) * 128)
                    if m >= 1:
                        ptp = PT[(i - 1) % NPT]
                        P.op("pe", lambda e, cs=cs, lhs_prev=lhs_prev, ptp=ptp: e.matmul(
                            ps[bank][:, cs], lhsT=lhs_prev, rhs=ptp[:, 128:256], start=True, stop=False),
                            reads=rk + [("PT", (i - 1) % NPT)], writes=pk(bank))
                    ptc = PT[i % NPT]
                    P.op("pe", lambda e, cs=cs, lhs_cur=lhs_cur, ptc=ptc: e.matmul(
                        ps[bank][:, cs], lhsT=lhs_cur, rhs=ptc[:, 0:128], start=(m == 0), stop=True),
                        reads=rk + [("PT", i % NPT)], writes=pk(bank))
                if slot == 1:
                    for (c0, dstbuf, dkeys, gname) in ((0, acc, ACCK, "acc"), (256, den, DENK, "den")):
                        if g == 0:
                            dv = dstbuf[:, bq * 256:(bq + 1) * 256]
                            sv = ps[bank][:, c0:c0 + 256]
                            kk = [dkeys[bq // 2]]
                        elif g == 1:
                            dv = dstbuf.rearrange("p (l r) -> p r l", r=4)[:, bq // 2, (bq % 2) * 256:(bq % 2 + 1) * 256]
                            sv = ps[bank][:, c0:c0 + 256]
                            kk = [(gname + "1", bq)]
                        else:
                            dv = dstbuf.rearrange("p (l r) -> p r l", r=16)[:, bq * 2:(bq + 1) * 2, :]
                            sv = ps[bank][:, c0:c0 + 256].rearrange("p (r l) -> p r l", r=2)
                            kk = [(gname + "2", bq)]
                        if g == 0:
                            P.op("dve", lambda e, dv=dv, sv=sv: e.tensor_copy(out=dv, in_=sv), reads=pk(bank), writes=kk)
                        else:
                            P.op("dve", lambda e, dv=dv, sv=sv: e.tensor_tensor(out=dv, in0=sv, in1=dv, op=ALU.add),
                                 reads=pk(bank) + kk, writes=kk)

            for i in range(16 + LAG):
                if i < 16:
                    qk_step(i)
                if i >= LAG:
                    pv_step(i - LAG)
                yield

        def merge_slot(j):
            P.alias(ACCK, [("acc2", x) for x in range(8)])
            P.alias(DENK, [("den2", x) for x in range(8)])
            for q in range(4):
                sl = slice(q * 512, (q + 1) * 512)
                P.op("dve", lambda e, sl=sl: e.reciprocal(out=den[:, sl], in_=den[:, sl]), reads=[("den", q)], writes=[("den", q)])
                P.op("dve", lambda e, sl=sl: e.tensor_tensor(out=acc[:, sl], in0=acc[:, sl], in1=den[:, sl], op=ALU.mult),
                     reads=[("acc", q), ("den", q)], writes=[("acc", q)])
                P.op("pool", lambda e, sl=sl, j=j: e.tensor_tensor(out=og[:, j, sl], in0=acc[:, sl], in1=gs[:, sl], op=ALU.mult),
                     reads=[("acc", q), ("gs", q)], writes=[("og", j)])

        drain(proj_gen(0, 0))
        drain(proj_gen(0, 1))
        if b == 0:
            P.dma("pool", lambda e: e.dma_start(out=wao, in_=wao_d.rearrange("(k p) c -> p k c", p=128)), "wao", writes=["wao"])
            P.dma("pool", lambda e: e.dma_start(out=wco, in_=wco_d.rearrange("(k p) c -> p k c", p=128)), "wco", writes=["wco"])
        Wfc = None
        for j in range(4):
            interleave(attn_gen(j, 0), chain(proj_gen(j, 2), gproj_gen(j)), 2, lead=8)
            interleave(attn_gen(j, 1), proj_gen(j + 1, 0) if j < 3 else iter(()), 2)
            if j == 3:
                Wfc = [A.alloc(R_OFF + i * 2048, 2048, [("Wfc", i)], BF16).rearrange("p (k c) -> p k c", k=8) for i in range(2)]
                load_wfc(Wfc[0], 0)
                load_wfc(Wfc[1], 1)
            interleave(attn_gen(j, 2), proj_gen(j + 1, 1) if j < 3 else iter(()), 2)
            merge_slot(j)
        if stop == "s1":
            raise _Stop()

        ro = R_OFF + 5120
        sg = A.alloc(ro, 8192, [("sg", fc) for fc in range(8)], BF16).rearrange("p (k t) -> p k t", k=8); ro += 8192
        zb = []
        for i in range(2):
            zb.append(A.alloc(ro, 2056, [("z", i)])); ro += 2056
        tmp3 = []
        for i in range(2):
            d_ = {}
            for nm in ("u", "y", "sgl"):
                d_[nm] = A.alloc(ro, 512, [(nm, i)]); ro += 512
            tmp3.append(d_)
        assert ro <= ARENA_WORDS
        for i in range(2):
            P.op("pool", lambda e, i=i: e.memset(zb[i][:, 0:2], 0.0), writes=[("z", i)])
        it3 = 0
        Wm = None
        for fc in range(8):
            W = Wfc[fc % 2]
            z = zb[fc % 2]
            zk = ("z", fc % 2)
            for q in range(4):
                T = tmp3[it3 % 2]
                ti = it3 % 2
                it3 += 1
                banks = [(it3 % 2) * 4 + s_ for s_ in range(4)]
                for s_ in (0, 3, 2, 1):
                    for k in range(KC):
                        P.op("pe", lambda e, s_=s_, k=k, W=W, q=q, bank=banks[s_]: e.matmul(
                            ps[bank][:, :], lhsT=W[:, k, s_ * 128:(s_ + 1) * 128], rhs=hT[:, k, q * 512:(q + 1) * 512],
                            start=(k == 0), stop=(k == KC - 1)), reads=[("Wfc", fc % 2), ("hT", q, k)], writes=pk(banks[s_]))
                bu, bb, bc, bg = banks
                zs = slice(2 + q * 512, 2 + (q + 1) * 512)
                P.op("act", lambda e, T=T, bu=bu: e.activation(out=T["u"], in_=ps[bu][:, :], func=AF.Identity),
                     reads=pk(bu), writes=[("u", ti)])
                P.op("act", lambda e, T=T, bg=bg: e.activation(out=T["sgl"], in_=ps[bg][:, :], func=AF.Silu),
                     reads=pk(bg), writes=[("sgl", ti)])
                P.op("dve", lambda e, T=T, bc=bc, z=z, zs=zs: e.tensor_tensor(out=z[:, zs], in0=ps[bc][:, :], in1=T["u"], op=ALU.mult),
                     reads=pk(bc) + [("u", ti)], writes=[zk])
                P.op("dve", lambda e, T=T, bb=bb: e.tensor_tensor(out=T["sgl"], in0=ps[bb][:, :], in1=T["sgl"], op=ALU.mult),
                     reads=pk(bb) + [("sgl", ti)], writes=[("sgl", ti)])
                P.op("act", lambda e, T=T, z=z, zs=zs, fc=fc: e.activation(
                    out=T["y"], in_=z[:, zs], func=AF.Identity, scale=cwT[:, fc * 4 + 2:fc * 4 + 3], bias=cwT[:, fc * 4 + 3:fc * 4 + 4]),
                    reads=[zk, "cwT"], writes=[("y", ti)])
                P.op("dve", lambda e, T=T, z=z, q=q, fc=fc: e.scalar_tensor_tensor(
                    out=T["y"], in0=z[:, 1 + q * 512:1 + (q + 1) * 512], scalar=cwT[:, fc * 4 + 1:fc * 4 + 2], in1=T["y"],
                    op0=ALU.mult, op1=ALU.add), reads=[zk, "cwT", ("y", ti)], writes=[("y", ti)])
                P.op("dve", lambda e, T=T, z=z, q=q, fc=fc: e.scalar_tensor_tensor(
                    out=T["y"], in0=z[:, q * 512:(q + 1) * 512], scalar=cwT[:, fc * 4 + 0:fc * 4 + 1], in1=T["y"],
                    op0=ALU.mult, op1=ALU.add), reads=[zk, "cwT", ("y", ti)], writes=[("y", ti)])
                P.op("pool", lambda e, T=T, fc=fc, q=q: e.tensor_tensor(out=sg[:, fc, q * 512:(q + 1) * 512], in0=T["y"], in1=T["sgl"],
                                                                        op=ALU.mult),
                     reads=[("y", ti), ("sgl", ti)], writes=[("sg", fc)])
            if fc + 2 < 8:
                load_wfc(W, fc + 2)
            if fc == 6:
                Wm = [A.alloc(R_OFF + i * 1024, 1024, [("Wm", i)], BF16).rearrange("p (k c) -> p k c", k=8) for i in range(2)]
                load_wm(Wm[0], 0)
                load_wm(Wm[1], 1)
        if stop == "s3":
            raise _Stop()

        merged = A.alloc(R_OFF + 13312, 8192, [("mg", q) for q in range(4)], BF16).rearrange("p (k t) -> p k t", k=8)
        ro = R_OFF + 2048
        tmp4 = []
        for i in range(2):
            d_ = {}
            for nm in ("sa", "sc"):
                d_[nm] = A.alloc(ro, 512, [(nm, i)]); ro += 512
            tmp4.append(d_)
        assert ro <= R_OFF + 5120
        SGK = [("sg", fc) for fc in range(8)]
        OGK = [("og", j) for j in range(4)]
        it4 = 0
        W1n = None
        for dc in range(8):
            W = Wm[dc % 2]
            for q in range(4):
                T = tmp4[it4 % 2]
                ti = it4 % 2
                it4 += 1
                bA, bMA, bS, bMC = [(it4 % 2) * 4 + s_ for s_ in range(4)]
                qs = slice(q * 512, (q + 1) * 512)
                for k in range(KC):
                    P.op("pe", lambda e, k=k, W=W, qs=qs, bMA=bMA: e.matmul(
                        ps[bMA][:, :], lhsT=W[:, k, 0:128], rhs=hT[:, k, qs], start=(k == 0), stop=(k == KC - 1)),
                        reads=[("Wm", dc % 2), ("hT", q, k)], writes=pk(bMA))
                for k in range(KC):
                    P.op("pe", lambda e, k=k, W=W, qs=qs, bMC=bMC: e.matmul(
                        ps[bMC][:, :], lhsT=W[:, k, 128:256], rhs=hT[:, k, qs], start=(k == 0), stop=(k == KC - 1)),
                        reads=[("Wm", dc % 2), ("hT", q, k)], writes=pk(bMC))
                for k in range(4):
                    P.op("pe", lambda e, k=k, qs=qs, bA=bA, dc=dc: e.matmul(
                        ps[bA][:, :], lhsT=wao[:, k, dc * 128:(dc + 1) * 128], rhs=og[:, k, qs], start=(k == 0), stop=(k == 3)),
                        reads=["wao"] + OGK, writes=pk(bA))
                for k in range(KC):
                    P.op("pe", lambda e, k=k, qs=qs, bS=bS, dc=dc: e.matmul(
                        ps[bS][:, :], lhsT=wco[:, k, dc * 128:(dc + 1) * 128], rhs=sg[:, k, qs], start=(k == 0), stop=(k == KC - 1)),
                        reads=["wco"] + SGK, writes=pk(bS))
                P.op("act", lambda e, T=T, bMA=bMA: e.activation(out=T["sa"], in_=ps[bMA][:, :], func=AF.Sigmoid),
                     reads=pk(bMA), writes=[("sa", ti)])
                P.op("act", lambda e, T=T, bMC=bMC: e.activation(out=T["sc"], in_=ps[bMC][:, :], func=AF.Sigmoid),
                     reads=pk(bMC), writes=[("sc", ti)])
                P.op("dve", lambda e, T=T, bA=bA: e.tensor_tensor(out=T["sa"], in0=ps[bA][:, :], in1=T["sa"], op=ALU.mult),
                     reads=pk(bA) + [("sa", ti)], writes=[("sa", ti)])
                P.op("dve", lambda e, T=T, bS=bS: e.tensor_tensor(out=T["sc"], in0=ps[bS][:, :], in1=T["sc"], op=ALU.mult),
                     reads=pk(bS) + [("sc", ti)], writes=[("sc", ti)])
                P.op("pool", lambda e, T=T, dc=dc, qs=qs: e.tensor_tensor(out=merged[:, dc, qs], in0=T["sa"], in1=T["sc"], op=ALU.add),
                     reads=[("sa", ti), ("sc", ti)], writes=[("mg", q)])
            if dc + 2 < 8:
                load_wm(W, dc + 2)
        if stop == "s4":
            raise _Stop()
        s0n = None
        if has_next:
            W1n = alloc_w1()
            for g_ in range(3):
                load_wg(W1n, 0, g_)
            load_wgate(W1n, 0)
            s0n = stage0_gen(b + 1, False)

        ro = R_OFF + 5120
        xr, rb, obuf = [], [], []
        for i in range(3):
            xr.append(A.alloc(ro, 1024, [("xr", i)])); ro += 1024
        for i in range(4):
            rb.append(A.alloc(ro, 1024, [("rb", i)])); ro += 1024
        obuf = rb
        stt = [A.alloc(ro + i * 16, 16, [("st", i)]) for i in range(4)]
        ro += 64
        mvt = [A.alloc(ro + i * 8, 8, [("mv", i)]) for i in range(4)]
        ro += 32
        assert ro <= R_OFF + 13312

        def xr_load(t):
            P.dma("sp", lambda e: e.dma_start(out=xr[t % 3], in_=x_d[b, t * 128:(t + 1) * 128, :]),
                  "xr%d" % (t % 3), writes=[("xr", t % 3)])
        xr_load(0)
        xr_load(1)
        def ph_A(t):
            i2, i3 = t % 4, t % 3
            banks = [(t % 4) * 2, (t % 4) * 2 + 1]
            for half in range(2):
                for k in range(KC):
                    P.op("pe", lambda e, k=k, half=half, bank=banks[half]: e.matmul(
                        ps[bank][:, :], lhsT=merged[:, k, t * 128:(t + 1) * 128], rhs=wo[:, k, half * 512:(half + 1) * 512],
                        start=(k == 0), stop=(k == KC - 1)), reads=["wo", ("mg", t // 4)], writes=pk(banks[half]))
            for half in range(2):
                hs = slice(half * 512, (half + 1) * 512)
                P.op("dve", lambda e, hs=hs, bank=banks[half]: e.scalar_tensor_tensor(
                    out=rb[i2][:, hs], in0=xr[i3][:, hs], scalar=ALPHA, in1=ps[bank][:, :], op0=ALU.mult, op1=ALU.add),
                    reads=pk(banks[half]) + [("xr", i3)], writes=[("rb", i2)])
            for c_ in range(2):
                P.op("dve", lambda e, c_=c_: e.bn_stats(out=stt[i2][:, c_ * 6:(c_ + 1) * 6], in_=rb[i2][:, c_ * 512:(c_ + 1) * 512]),
                     reads=[("rb", i2)], writes=[("st", i2)])
            mv = mvt[i2]
            P.op("dve", lambda e: e.bn_aggr(out=mv[:, 0:2], in_=stt[i2][:, 0:12]), reads=[("st", i2)], writes=[("mv", i2)])
            P.op("act", lambda e: e.activation(out=mv[:, 2:3], in_=mv[:, 1:2], func=AF.Sqrt, bias=epst[:, 0:1], scale=1.0),
                 reads=[("mv", i2), "eps"], writes=[("mv", i2)])

        def ph_C(t):
            i2 = t % 4
            mv = mvt[i2]
            P.op("dve", lambda e: e.reciprocal(out=mv[:, 3:4], in_=mv[:, 2:3]), reads=[("mv", i2)], writes=[("mv", i2)])
            P.op("dve", lambda e: e.scalar_tensor_tensor(out=mv[:, 4:5], in0=mv[:, 0:1], scalar=-1.0, in1=mv[:, 3:4],
                                                         op0=ALU.mult, op1=ALU.mult),
                 reads=[("mv", i2)], writes=[("mv", i2)])
            P.op("act", lambda e: e.activation(out=obuf[i2], in_=rb[i2], func=AF.Identity, scale=mv[:, 3:4], bias=mv[:, 4:5]),
                 reads=[("rb", i2), ("mv", i2)], writes=[("rb", i2)])

        def ph_E(t):
            i2 = t % 4
            P.op("dve", lambda e: e.tensor_tensor(out=obuf[i2], in0=obuf[i2], in1=lng, op=ALU.mult),
                 reads=[("rb", i2), "lng"], writes=[("rb", i2)])
            P.op("pool", lambda e: e.tensor_tensor(out=obuf[i2], in0=obuf[i2], in1=lnb, op=ALU.add),
                 reads=[("rb", i2), "lnb"], writes=[("rb", i2)])
            P.dma("pool", lambda e: e.dma_start(out=y_d[b, t * 128:(t + 1) * 128, :], in_=obuf[i2]),
                  "yo%d" % i2, reads=[("rb", i2)])

        for t in range(16 + 2):
            if t < 16:
                if t + 2 < 16:
                    xr_load(t + 2)
                ph_A(t)
            if 0 <= t - 1 < 16:
                ph_C(t - 1)
            if 0 <= t - 2 < 16:
                ph_E(t - 2)
            if s0n is not None and t < 16:
                next(s0n, None)
        if s0n is not None:
            drain(s0n)
        return W1n

    W1c = alloc_w1()
    for g_ in range(3):
        load_wg(W1c, 0, g_)
    load_wgate(W1c, 0)
    try:
        if stop != "pro":
            drain(stage0_gen(0, True))
            late_consts()
            compute_gate1()
            if stop == "s0":
                raise _Stop()
            nseq = NSEQ if stop is None else 1
            for b_ in range(nseq):
                W1c = seq_body(b_, W1c, b_ + 1 < nseq)
    except _Stop:
        pass

    P.emit()
    return nc, dbg_outs


def prep_inputs(x, c, w_ada, b_ada, w_in, conv_w, conv_b, rel_bias, w_attn_out, w_conv_out, w_o, ln_g, ln_b):
    f = lambda a: np.ascontiguousarray(np.asarray(a, dtype=np.float32))
    x = f(x); c = f(c)
    perm = _w_in_perm()
    w_in_p = f(np.asarray(w_in)[0][:, perm])
    b_ada0 = np.asarray(b_ada, dtype=np.float32)[0]
    badaT = f(b_ada0[:2048].reshape(16, 128).T)
    bgate = f(np.broadcast_to(b_ada0[2048:][None, :], (128, D)))
    cw = np.asarray(conv_w, dtype=np.float32)[0]
    cbv = np.asarray(conv_b, dtype=np.float32)[0]
    cw4 = np.concatenate([cw, cbv[None, :]], axis=0)
    cwT = f(cw4.reshape(4, 8, 128).transpose(2, 1, 0).reshape(128, 32))
    idx, mask = _bias_index_and_mask()
    rb = np.asarray(rel_bias, dtype=np.float32)
    eb = np.empty((128, 12, 256), np.float32)
    for g in range(3):
        for j in range(4):
            h = g * 4 + j
            eb[:, h, :] = rb[:, h][idx[g]]
    shared = {
        "w_ada": f(np.asarray(w_ada)[0]), "badaT": badaT, "bgate": bgate, "w_in": w_in_p, "cwT": cwT,
        "ebsrc": f(eb.reshape(128, 12 * 256)), "mask": f(mask), "ident": np.eye(128, dtype=np.float32),
        "w_attn_out": f(np.asarray(w_attn_out)[0]), "w_conv_out": f(np.asarray(w_conv_out)[0]), "w_o": f(np.asarray(w_o)[0]),
        "lng": f(np.broadcast_to(np.asarray(ln_g, dtype=np.float32)[0][None, :], (128, D))),
        "lnb": f(np.broadcast_to(np.asarray(ln_b, dtype=np.float32)[0][None, :], (128, D))),
    }
    in_maps = []
    for i in range(N_CORES):
        m = dict(shared)
        m["x"] = f(x[i * NSEQ:(i + 1) * NSEQ])
        cc = c[i * NSEQ:(i + 1) * NSEQ]
        m["cT"] = f(cc.reshape(NSEQ, 8, 128).transpose(2, 1, 0).reshape(128, 16))
        in_maps.append(m)
    return in_maps


_NC_CACHE = {}


def kernel(x, c, w_ada, b_ada, w_in, conv_w, conv_b, rel_bias, w_attn_out, w_conv_out, w_o, ln_g, ln_b):
    in_maps = prep_inputs(x, c, w_ada, b_ada, w_in, conv_w, conv_b, rel_bias, w_attn_out, w_conv_out, w_o, ln_g, ln_b)
    if "nc" not in _NC_CACHE:
        _NC_CACHE["nc"] = build_program()[0]
    nc = _NC_CACHE["nc"]
    res = run_bass_kernel_spmd(nc, in_maps, core_ids=list(range(N_CORES)))
    out = np.concatenate([np.asarray(r["y"], dtype=np.float32) for r in res.results], axis=0)
    return out.reshape(N_CORES * NSEQ, S, D)
```

```python
import contextlib
import math
import os

import numpy as np

import concourse.bass as bass
import concourse.mybir as mybir
from concourse.bass_utils import run_bass_kernel_spmd

F32 = mybir.dt.float32
BF16 = mybir.dt.bfloat16
AF = mybir.ActivationFunctionType
ALU = mybir.AluOpType

N_CORES = 8
S = 2048
D = 1024
NSEQ = 2
KC = 8
N_COLS = 11264
ALPHA = 2.0 ** 0.25
LN_EPS = 1e-5
QK_SCALE = 128.0 ** -0.5
GROUPS = ((128, 1), (512, 4), (2048, 16))
ARENA_WORDS = 53000

ENGS = ("pe", "act", "dve", "pool", "sp")


class Prog:
    def __init__(self, nc):
        self.nc = nc
        self.ops = {e: [] for e in ENGS}
        self.state = {}
        self.dma_count = {}

    def _deps(self, eng, reads, writes):
        deps = set()
        for k in reads:
            st = self.state.get(k)
            if st is not None:
                deps.update(st[0])
        for k in writes:
            st = self.state.get(k)
            if st is not None:
                for w in st[0]:
                    if not (w[0] == "eng" and w[1] == eng and eng != "pool"):
                        deps.add(w)
                for r in st[1]:
                    if not (r[0] == "eng" and r[1] == eng and eng != "pool"):
                        deps.add(r)
        return deps

    def _commit(self, me, reads, writes):
        for k in reads:
            st = self.state.get(k)
            if st is None:
                self.state[k] = [[], [me]]
            else:
                st[1].append(me)
        for k in writes:
            self.state[k] = [[me], []]

    def op(self, eng, fn, reads=(), writes=()):
        deps = self._deps(eng, reads, writes)
        me = ("eng", eng, len(self.ops[eng]))
        self.ops[eng].append({"fn": fn, "deps": deps, "slot": None, "inc": False})
        self._commit(me, reads, writes)

    def dma(self, queue, fn, slot, reads=(), writes=()):
        deps = self._deps(None, reads, writes)
        cnt = self.dma_count.get(slot, 0) + 1
        self.dma_count[slot] = cnt
        if cnt > 1:
            deps.add(("dma", slot, 16 * (cnt - 1)))
        me = ("dma", slot, 16 * cnt)
        self.ops[queue].append({"fn": fn, "deps": deps, "slot": slot, "inc": False})
        self._commit(me, reads, writes)

    def alias(self, new_keys, old_keys):
        ws, rs = [], []
        for k in old_keys:
            st = self.state.get(k)
            if st is not None:
                ws.extend(st[0])
                rs.extend(st[1])
        ws = list(dict.fromkeys(ws))
        rs = list(dict.fromkeys(rs))
        for k in new_keys:
            self.state[k] = [list(ws), list(rs)]

    def emit(self):
        nc = self.nc
        for e in ENGS:
            for o in self.ops[e]:
                for d in o["deps"]:
                    if d[0] == "eng":
                        self.ops[d[1]][d[2]]["inc"] = True
        mile = {}
        for e in ENGS:
            c = 0
            for i, o in enumerate(self.ops[e]):
                if o["slot"] is None and o["inc"]:
                    c += 1
                    mile[(e, i)] = c
        slots = sorted(self.dma_count.keys())
        with contextlib.ExitStack() as es:
            esem = {e: es.enter_context(nc.semaphore("prog_" + e)) for e in ENGS}
            dsem = {s: es.enter_context(nc.semaphore("dma_" + s)) for s in slots}
            block = es.enter_context(nc.Block())

            def run(e, eng):
                seen = {}
                for o in self.ops[e]:
                    waits = {}
                    for d in o["deps"]:
                        if d[0] == "eng":
                            key = ("e", d[1])
                            val = mile[(d[1], d[2])]
                            sem = esem[d[1]]
                        else:
                            key = ("d", d[1])
                            val = d[2]
                            sem = dsem[d[1]]
                        if seen.get(key, 0) >= val:
                            continue
                        if key not in waits or waits[key][1] < val:
                            waits[key] = (sem, val)
                    for key, (sem, val) in waits.items():
                        eng.wait_ge(sem, val)
                        seen[key] = val
                    ins = o["fn"](eng)
                    if o["slot"] is not None:
                        ins.then_inc(dsem[o["slot"]], 16)
                    elif o["inc"]:
                        ins.then_inc(esem[e], 1)
                if e == "sp":
                    for s in slots:
                        eng.wait_ge(dsem[s], 16 * self.dma_count[s])

            @block.tensor
            def _(eng):
                run("pe", eng)

            @block.scalar
            def _(eng):
                run("act", eng)

            @block.vector
            def _(eng):
                run("dve", eng)

            @block.gpsimd
            def _(eng):
                run("pool", eng)

            @block.sync
            def _(eng):
                run("sp", eng)


class _Stop(Exception):
    pass


class Arena:
    def __init__(self, P, ap):
        self.P = P
        self.ap = ap
        self.bufs = []

    def alloc(self, off, nwords, keys, dt=F32):
        assert off >= 0 and off + nwords <= ARENA_WORDS, (off, nwords)
        old = []
        for (o2, n2, k2) in self.bufs:
            if o2 < off + nwords and off < o2 + n2:
                old.extend(k2)
        keys = list(keys)
        if old:
            self.P.alias(keys, old)
        self.bufs.append((off, nwords, keys))
        v = self.ap[:, off:off + nwords]
        return v if dt == F32 else v.bitcast(dt)


def _w_in_perm():
    perm = []
    for j in range(4):
        for g in range(3):
            h = g * 4 + j
            for sec in range(3):
                perm.extend(range(sec * 1536 + h * 128, sec * 1536 + (h + 1) * 128))
        perm.extend(range(4608 + j * 128, 4608 + (j + 1) * 128))
    for fc in range(8):
        for sec in range(4):
            perm.extend(range(5120 + sec * 1024 + fc * 128, 5120 + sec * 1024 + (fc + 1) * 128))
    for dc in range(8):
        for sec in range(2):
            perm.extend(range(9216 + sec * 1024 + dc * 128, 9216 + sec * 1024 + (dc + 1) * 128))
    perm = np.asarray(perm, dtype=np.int64)
    assert perm.shape[0] == N_COLS and np.unique(perm).shape[0] == N_COLS
    return perm


def _t5_bucket(dist):
    dist = np.asarray(dist, dtype=np.int32)
    n = np.maximum(dist, 1).astype(np.float32)
    large = 16 + (np.log(n / np.float32(16)) / np.float32(math.log(2048 / 16)) * np.float32(16)).astype(np.int32)
    large = np.minimum(large, 31)
    return np.where(dist < 16, dist, large)


def _bias_index_and_mask():
    b = np.arange(128)[:, None]
    a = np.arange(128)[None, :]
    steps_cur = a - b
    steps_prev = a + 128 - b
    mask = np.concatenate([(steps_cur >= 0), (steps_prev <= 128)], axis=1).astype(np.float32)
    idx = []
    for (_, dil) in GROUPS:
        cur = _t5_bucket(np.maximum(steps_cur, 0) * dil)
        prev = _t5_bucket(np.clip(steps_prev, 0, 128) * dil)
        idx.append(np.concatenate([cur, prev], axis=1))
    return np.stack(idx, 0), mask


def build_program(debug=None, stop=None):
    nc = bass.Bass("TRN2", target_bir_lowering=False)

    def din(name, shape):
        return nc.dram_tensor(name, list(shape), F32, kind="ExternalInput").ap()

    x_d = din("x", [NSEQ, S, D])
    cT_d = din("cT", [128, 16])
    wada_d = din("w_ada", [D, 3 * D])
    badaT_d = din("badaT", [128, 16])
    bgate_d = din("bgate", [128, D])
    win_d = din("w_in", [D, N_COLS])
    cwT_d = din("cwT", [128, 32])
    eb_d = din("ebsrc", [128, 12 * 256])
    mask_d = din("mask", [128, 256])
    ident_d = din("ident", [128, 128])
    wao_d = din("w_attn_out", [512, D])
    wco_d = din("w_conv_out", [D, D])
    wo_d = din("w_o", [D, D])
    lng_d = din("lng", [128, D])
    lnb_d = din("lnb", [128, D])
    y_d = nc.dram_tensor("y", [NSEQ, S, D], F32, kind="ExternalOutput").ap()

    win_v = win_d.rearrange("(k p) c -> p k c", p=128)
    wada_v = wada_d.rearrange("(k p) c -> p k c", p=128)

    arena_ap = nc.alloc_sbuf_tensor("arena", [128, ARENA_WORDS], F32).ap()
    ps = [nc.alloc_psum_tensor("psb%d" % i, [128, 512], F32).ap() for i in range(8)]

    P = Prog(nc)
    A = Arena(P, arena_ap)

    def pk(b):
        return [("ps", b, 0), ("ps", b, 1)]

    off = [0]

    def palloc(n, keys, dt=F32):
        v = A.alloc(off[0], n, keys, dt)
        off[0] += n
        return v

    ident = palloc(128, ["ident"])
    ones_bf = palloc(64, ["ones"], BF16)
    EB = palloc(3072, ["EB"])
    maskm = palloc(256, ["mask"])
    cwT = palloc(32, ["cwT"])
    badaT = palloc(16, ["badaT"])
    scT = palloc(16, ["scT"])
    shT = palloc(16, ["shT"])
    cT = palloc(16, ["cT"])
    scb = palloc(8, ["scb"], BF16)
    epst = palloc(8, ["eps"])
    gate1 = palloc(2048, ["gate1"])
    bgate = palloc(1024, ["bgate"])
    lng = palloc(1024, ["lng"])
    lnb = palloc(1024, ["lnb"])
    wao = palloc(2048, ["wao"], BF16).rearrange("p (k c) -> p k c", k=4)
    wco = palloc(4096, ["wco"], BF16).rearrange("p (k c) -> p k c", k=8)
    wo = palloc(4096, ["wo"], BF16).rearrange("p (k c) -> p k c", k=8)
    HT_OFF = off[0]
    OG_OFF = HT_OFF + 8192
    R_OFF = OG_OFF + 4096
    R_WORDS = ARENA_WORDS - R_OFF
    assert R_WORDS >= 21504, R_WORDS

    sp_n = [0]

    def load(dst, src, key, queue="sp"):
        sp_n[0] += 1
        P.dma(queue, lambda e: e.dma_start(out=dst, in_=src), "c%d" % sp_n[0], writes=[key])

    load(ident, ident_d, "ident")
    load(cT, cT_d, "cT")
    load(badaT, badaT_d, "badaT")
    P.op("pool", lambda e: e.memset(ones_bf, 1.0), writes=["ones"])
    P.op("pool", lambda e: e.memset(epst, LN_EPS), writes=["eps"])

    wa = [A.alloc(R_OFF + i * 4096, 4096, [("wa", i)], BF16).rearrange("p (k c) -> p k c", k=8) for i in range(3)]
    Lb = [A.alloc(R_OFF + 12288 + i * 512, 512, [("Lb", i)], BF16).rearrange("p (k c) -> p k c", k=8) for i in range(2)]
    for i in range(3):
        P.dma("pool", lambda e, i=i: e.dma_start(out=wa[i], in_=wada_v[:, :, i * 1024:(i + 1) * 1024]),
              "wa%d" % i, writes=[("wa", i)])

    P.op("act", lambda e: e.activation(out=scb, in_=cT, func=AF.Silu), reads=["cT"], writes=["scb"])
    for b in range(NSEQ):
        for k in range(KC):
            P.op("dve", lambda e, b=b, k=k: e.tensor_copy(out=Lb[b][:, k, :],
                                                          in_=scb[:, 2 * k + b:2 * k + b + 1].to_broadcast([128, 128])),
                 reads=["scb"], writes=[("Lb", b)])
    for i, (dst, bank) in enumerate(((shT, 0), (scT, 1))):
        for j in range(KC):
            for k in range(KC):
                P.op("pe", lambda e, i=i, j=j, k=k, bank=bank: e.matmul(
                    ps[bank][:, 2 * j:2 * j + 2], lhsT=wa[i][:, k, j * 128:(j + 1) * 128], rhs=scb[:, 2 * k:2 * k + 2],
                    start=(k == 0), stop=(k == KC - 1)),
                    reads=[("wa", i), "scb"], writes=pk(bank))
        for b in range(NSEQ):
            if i == 0:
                P.op("dve", lambda e, b=b, bank=bank: e.tensor_tensor(
                    out=shT[:, b * 8:(b + 1) * 8], in0=ps[bank][:, b:16:2], in1=badaT[:, 0:8], op=ALU.add),
                    reads=pk(bank) + ["badaT"], writes=[("shT", b)])
            else:
                P.op("dve", lambda e, b=b, bank=bank: e.scalar_tensor_tensor(
                    out=scT[:, b * 8:(b + 1) * 8], in0=ps[bank][:, b:16:2], scalar=1.0, in1=badaT[:, 8:16],
                    op0=ALU.add, op1=ALU.add),
                    reads=pk(bank) + ["badaT"], writes=[("scT", b)])
    def late_consts():
        load(bgate, bgate_d, "bgate")
        load(EB, eb_d, "EB")
        load(maskm, mask_d, "mask")
        load(cwT, cwT_d, "cwT")
        load(lng, lng_d, "lng")
        load(lnb, lnb_d, "lnb")
        P.op("act", lambda e: e.activation(out=EB, in_=EB, func=AF.Exp), reads=["EB"], writes=["EB"])
        for h in range(12):
            P.op("dve", lambda e, h=h: e.tensor_tensor(out=EB[:, h * 256:(h + 1) * 256], in0=EB[:, h * 256:(h + 1) * 256],
                                                       in1=maskm, op=ALU.mult),
                 reads=["EB", "mask"], writes=[("EBm", h)])

    def compute_gate1():
        for b in range(NSEQ):
            for half in range(2):
                bank = 2 + b * 2 + half
                for k in range(KC):
                    P.op("pe", lambda e, b=b, half=half, k=k, bank=bank: e.matmul(
                        ps[bank][:, :], lhsT=Lb[b][:, k, :], rhs=wa[2][:, k, half * 512:(half + 1) * 512],
                        start=(k == 0), stop=(k == KC - 1)),
                        reads=[("Lb", b), ("wa", 2)], writes=pk(bank))
                P.op("dve", lambda e, b=b, half=half, bank=bank: e.scalar_tensor_tensor(
                    out=gate1[:, b * 1024 + half * 512: b * 1024 + (half + 1) * 512], in0=ps[bank][:, :], scalar=1.0,
                    in1=bgate[:, half * 512:(half + 1) * 512], op0=ALU.add, op1=ALU.add),
                    reads=pk(bank) + ["bgate"], writes=[("gate1", b)])


    bank_rr = [0]

    def next_bank(lo=0, n=8):
        b = lo + bank_rr[0] % n
        bank_rr[0] += 1
        return b

    dbg_outs = {}

    def dbg(name, ap, nwords_shape, reads):
        if debug is None or name not in debug:
            return
        t = nc.dram_tensor("dbg_" + name, list(nwords_shape), ap.dtype, kind="ExternalOutput").ap()
        dbg_outs[name] = t
        P.dma("sp", lambda e: e.dma_start(out=t, in_=ap), "dbg_" + name, reads=reads)

    def alloc_w1():
        WG = [A.alloc(R_OFF + g * 1536, 1536, [("WG", g)], BF16).rearrange("p (k c) -> p k c", k=8) for g in range(3)]
        Wg = A.alloc(R_OFF + 4608, 512, ["Wg"], BF16).rearrange("p (k c) -> p k c", k=8)
        return WG, Wg

    def load_wg(W1, j, g):
        cb = j * 1280 + g * 384
        buf = W1[0][g]
        P.dma("pool", lambda e: e.dma_start(out=buf, in_=win_v[:, :, cb:cb + 384]), "WG%d" % g, writes=[("WG", g)])

    def load_wgate(W1, j):
        cb = j * 1280 + 1152
        buf = W1[1]
        P.dma("pool", lambda e: e.dma_start(out=buf, in_=win_v[:, :, cb:cb + 128]), "Wg", writes=["Wg"])

    def load_wfc(Wbuf, fc):
        cbase = 5120 + fc * 512
        P.dma("pool", lambda e: e.dma_start(out=Wbuf, in_=win_v[:, :, cbase:cbase + 512]),
              "Wfc%d" % (fc % 2), writes=[("Wfc", fc % 2)])

    def load_wm(Wbuf, dc):
        cbase = 9216 + dc * 256
        P.dma("pool", lambda e: e.dma_start(out=Wbuf, in_=win_v[:, :, cbase:cbase + 256]),
              "Wm%d" % (dc % 2), writes=[("Wm", dc % 2)])

    HTK = [("hT", q, k) for q in range(4) for k in range(KC)]
    ACCK = [("acc", q) for q in range(4)]
    DENK = [("den", q) for q in range(4)]
    LAG = 4
    NE, NPT = 3, 6

    def drain(gen):
        for _ in gen:
            pass

    def interleave(ga, gb, nb, lead=0):
        a_done = b_done = False
        for _ in range(lead):
            try:
                next(gb)
            except StopIteration:
                b_done = True
                break
        while not (a_done and b_done):
            if not a_done:
                try:
                    next(ga)
                except StopIteration:
                    a_done = True
            for _ in range(nb if not a_done else 1000000):
                if b_done:
                    break
                try:
                    next(gb)
                except StopIteration:
                    b_done = True

    def chain(*gens):
        for g_ in gens:
            if g_ is not None:
                yield from g_

    def stage0_gen(b, split_evac):
        hT = A.alloc(HT_OFF, 8192, HTK, BF16).rearrange("p (k t) -> p k t", k=8)
        xs = [A.alloc(OG_OFF + i * 1024, 1024, [("xs", i)]) for i in range(4)]

        def xs_load(t):
            xb = xs[t % 4]
            P.dma("sp", lambda e: e.dma_start(out=xb, in_=x_d[b, t * 128:(t + 1) * 128, :]),
                  "xs%d" % (t % 4), writes=[("xs", t % 4)])
        xs_load(0)
        xs_load(1)
        xs_load(2)
        for t in range(16):
            if t + 3 < 16:
                xs_load(t + 3)
            xb = xs[t % 4]
            for kq in range(2):
                bank = next_bank()
                for k4 in range(4):
                    k = kq * 4 + k4
                    P.op("pe", lambda e, xb=xb, k=k, k4=k4, bank=bank: e.transpose(
                        ps[bank][:, k4 * 128:(k4 + 1) * 128], xb[:, k * 128:(k + 1) * 128], ident),
                        reads=[("xs", t % 4), "ident"], writes=pk(bank))
                for k4 in range(4):
                    k = kq * 4 + k4
                    if split_evac and kq == 1:
                        P.op("dve", lambda e, k=k, k4=k4, bank=bank, t=t: e.tensor_scalar(
                            out=hT[:, k, t * 128:(t + 1) * 128], in0=ps[bank][:, k4 * 128:(k4 + 1) * 128],
                            scalar1=scT[:, b * 8 + k:b * 8 + k + 1], scalar2=shT[:, b * 8 + k:b * 8 + k + 1],
                            op0=ALU.mult, op1=ALU.add),
                            reads=pk(bank) + [("scT", b), ("shT", b)], writes=[("hT", t // 4, k)])
                    else:
                        P.op("act", lambda e, k=k, k4=k4, bank=bank, t=t: e.activation(
                            out=hT[:, k, t * 128:(t + 1) * 128], in_=ps[bank][:, k4 * 128:(k4 + 1) * 128], func=AF.Identity,
                            scale=scT[:, b * 8 + k:b * 8 + k + 1], bias=shT[:, b * 8 + k:b * 8 + k + 1]),
                            reads=pk(bank) + [("scT", b), ("shT", b)], writes=[("hT", t // 4, k)])
            yield

    def seq_body(b, W1, has_next):
        hT = arena_ap[:, HT_OFF:HT_OFF + 8192].bitcast(BF16).rearrange("p (k t) -> p k t", k=8)
        og = A.alloc(OG_OFF, 4096, [("og", j) for j in range(4)], BF16).rearrange("p (j t) -> p j t", j=4)
        WG, Wg = W1
        wstg = arena_ap[:, OG_OFF + 3072:OG_OFF + 4096]
        wo_v = wo_d.rearrange("(k p) c -> p k c", p=128)
        for k in range(KC):
            P.dma("sp", lambda e, k=k: e.dma_start(out=wstg, in_=wo_v[:, k, :]), "wstg", writes=[("og", 3)])
            P.op("pool", lambda e, k=k: e.tensor_tensor(out=wo[:, k, :], in0=wstg, in1=gate1[:, b * 1024:(b + 1) * 1024], op=ALU.mult),
                 reads=[("og", 3), ("gate1", b)], writes=["wo"])
        ro = R_OFF + 5120
        QT = A.alloc(ro, 3072, [("QT", g, q) for g in range(3) for q in range(4)], BF16).rearrange("p (g t) -> p g t", g=3); ro += 3072
        KT = A.alloc(ro, 3072, [("KT", g, q) for g in range(3) for q in range(4)], BF16).rearrange("p (g t) -> p g t", g=3); ro += 3072
        V = A.alloc(ro, 3072, [("V", g, q) for g in range(3) for q in range(4)], BF16).rearrange("p (g t) -> p g t", g=3); ro += 3072
        gs = A.alloc(ro, 1024, [("gs", q) for q in range(4)], BF16); ro += 1024
        acc = A.alloc(ro, 2048, ACCK); ro += 2048
        den = A.alloc(ro, 2048, DENK); ro += 2048
        Eb = []
        for i in range(NE):
            Eb.append(A.alloc(ro, 256, [("E", i)])); ro += 256
        PT = []
        for i in range(NPT):
            PT.append(A.alloc(ro, 128, [("PT", i)], BF16)); ro += 128
        assert ro <= ARENA_WORDS

        def proj_gen(j, g):
            Wb = WG[g]
            wk = ("WG", g)
            for q in range(4):
                bank = next_bank(0, 2)
                for k in range(KC):
                    P.op("pe", lambda e, q=q, k=k, bank=bank: e.matmul(
                        ps[bank][:, :], lhsT=Wb[:, k, 0:128], rhs=hT[:, k, q * 512:(q + 1) * 512],
                        start=(k == 0), stop=(k == KC - 1)), reads=[wk, ("hT", q, k)], writes=pk(bank))
                P.op("act", lambda e, q=q, bank=bank: e.activation(out=QT[:, g, q * 512:(q + 1) * 512], in_=ps[bank][:, :],
                                                                   func=AF.Identity),
                     reads=pk(bank), writes=[("QT", g, q)])
                yield
            for q in range(4):
                bank = next_bank(0, 2)
                for k in range(KC):
                    P.op("pe", lambda e, q=q, k=k, bank=bank: e.matmul(
                        ps[bank][:, :], lhsT=Wb[:, k, 128:256], rhs=hT[:, k, q * 512:(q + 1) * 512],
                        start=(k == 0), stop=(k == KC - 1)), reads=[wk, ("hT", q, k)], writes=pk(bank))
                P.op("act", lambda e, q=q, bank=bank: e.activation(out=KT[:, g, q * 512:(q + 1) * 512], in_=ps[bank][:, :],
                                                                   func=AF.Identity),
                     reads=pk(bank), writes=[("KT", g, q)])
                yield
            dil = GROUPS[g][1]
            nblk = 16 // dil
            for b4 in range(4):
                bank = next_bank(0, 2)
                for i4 in range(4):
                    blk = b4 * 4 + i4
                    r, n = blk // nblk, blk % nblk
                    for k in range(KC):
                        hsub = hT[:, k, :].rearrange("p (l r) -> p r l", r=dil)
                        P.op("pe", lambda e, k=k, hsub=hsub, r=r, n=n, i4=i4, bank=bank: e.matmul(
                            ps[bank][:, i4 * 128:(i4 + 1) * 128], lhsT=hsub[:, r, n * 128:(n + 1) * 128],
                            rhs=Wb[:, k, 256:384], start=(k == 0), stop=(k == KC - 1)),
                            reads=[wk] + [("hT", q_, k) for q_ in range(4)], writes=pk(bank))
                    if i4 < 3:
                        yield
                if False:
                    pass
                else:
                    P.op("act", lambda e, b4=b4, bank=bank: e.activation(out=V[:, g, b4 * 512:(b4 + 1) * 512], in_=ps[bank][:, :],
                                                                         func=AF.Identity),
                         reads=pk(bank), writes=[("V", g, b4)])
                yield
            if j < 3:
                load_wg(W1, j + 1, g)

        def gproj_gen(j):
            for q in range(4):
                bank = next_bank(0, 2)
                for k in range(KC):
                    P.op("pe", lambda e, q=q, k=k, bank=bank: e.matmul(
                        ps[bank][:, :], lhsT=Wg[:, k, :], rhs=hT[:, k, q * 512:(q + 1) * 512],
                        start=(k == 0), stop=(k == KC - 1)), reads=["Wg", ("hT", q, k)], writes=pk(bank))
                P.op("act", lambda e, q=q, bank=bank: e.activation(out=gs[:, q * 512:(q + 1) * 512], in_=ps[bank][:, :], func=AF.Silu),
                     reads=pk(bank), writes=[("gs", q)])
                yield
            if j < 3:
                load_wgate(W1, j + 1)

        def attn_gen(j, g):
            dil = GROUPS[g][1]
            nblk = 16 // dil
            h = g * 4 + j
            QTg = QT[:, g, :].rearrange("p (l r) -> p r l", r=dil)
            KTg = KT[:, g, :].rearrange("p (l r) -> p r l", r=dil)
            QK_KEYS = [("QT", g, q) for q in range(4)] + [("KT", g, q) for q in range(4)]
            if g == 1:
                P.alias([("acc1", x) for x in range(8)], ACCK)
                P.alias([("den1", x) for x in range(8)], DENK)
            elif g == 2:
                P.alias([("acc2", x) for x in range(8)], [("acc1", x) for x in range(8)])
                P.alias([("den2", x) for x in range(8)], [("den1", x) for x in range(8)])

            def qk_step(i):
                r, m = i // nblk, i % nblk
                nq = 256 if m < nblk - 1 else 128
                sb = 2 + i % 4
                P.op("pe", lambda e: e.matmul(
                    ps[sb][:, 0:nq], lhsT=KTg[:, r, m * 128:(m + 1) * 128],
                    rhs=QTg[:, r, m * 128:m * 128 + nq], start=True, stop=True),
                    reads=QK_KEYS, writes=pk(sb))
                E = Eb[i % NE]
                P.op("act", lambda e: e.activation(out=E[:, 0:nq], in_=ps[sb][:, 0:nq], func=AF.Exp, scale=QK_SCALE),
                     reads=pk(sb), writes=[("E", i % NE)])
                pt = PT[i % NPT]
                P.op("pool", lambda e: e.tensor_tensor(out=pt[:, 0:nq], in0=E[:, 0:nq], in1=EB[:, h * 256:h * 256 + nq],
                                                      op=ALU.mult),
                     reads=[("E", i % NE), ("EBm", h)], writes=[("PT", i % NPT)])

            def pv_step(i):
                r, m = i // nblk, i % nblk
                if g == 0:
                    bq, slot = m // 2, m % 2
                elif g == 1:
                    bq, slot = r * 2 + m // 2, m % 2
                else:
                    bq, slot = r // 2, r % 2
                bank = 6 + bq % 2
                vkeys = [("V", g, q) for q in range(4)]
                vprev = V[:, g, (i - 1) * 128:i * 128] if m >= 1 else None
                vcur = V[:, g, i * 128:(i + 1) * 128]
                for (c0, lhs_prev, lhs_cur, rk) in ((0, vprev, vcur, vkeys), (256, ones_bf, ones_bf, ["ones"])):
                    cs = slice(c0 + slot * 128, c0 + (slot + 1) * 128)
                    if m >= 1:
                        ptp = PT[(i - 1) % NPT]
                        P.op("pe", lambda e, cs=cs, lhs_prev=lhs_prev, ptp=ptp: e.matmul(
                            ps[bank][:, cs], lhsT=lhs_prev, rhs=ptp[:, 128:256], start=True, stop=False),
                            reads=rk + [("PT", (i - 1) % NPT)], writes=pk(bank))
                    ptc = PT[i % NPT]
                    P.op("pe", lambda e, cs=cs, lhs_cur=lhs_cur, ptc=ptc: e.matmul(
                        ps[bank][:, cs], lhsT=lhs_cur, rhs=ptc[:, 0:128], start=(m == 0), stop=True),
                        reads=rk + [("PT", i % NPT)], writes=pk(bank))
                if slot == 1:
                    for (c0, dstbuf, dkeys, gname) in ((0, acc, ACCK, "acc"), (256, den, DENK, "den")):
                        if g == 0:
                            dv = dstbuf[:, bq * 256:(bq + 1) * 256]
                            sv = ps[bank][:, c0:c0 + 256]
                            kk = [dkeys[bq // 2]]
                        elif g == 1:
                            dv = dstbuf.rearrange("p (l r) -> p r l", r=4)[:, bq // 2, (bq % 2) * 256:(bq % 2 + 1) * 256]
                            sv = ps[bank][:, c0:c0 + 256]
                            kk = [(gname + "1", bq)]
                        else:
                            dv = dstbuf.rearrange("p (l r) -> p r l", r=16)[:, bq * 2:(bq + 1) * 2, :]
                            sv = ps[bank][:, c0:c0 + 256].rearrange("p (r l) -> p r l", r=2)
                            kk = [(gname + "2", bq)]
                        if g == 0:
                            P.op("dve", lambda e, dv=dv, sv=sv: e.tensor_copy(out=dv, in_=sv), reads=pk(bank), writes=kk)
                        else:
                            P.op("dve", lambda e, dv=dv, sv=sv: e.tensor_tensor(out=dv, in0=sv, in1=dv, op=ALU.add),
                                 reads=pk(bank) + kk, writes=kk)

            for i in range(16 + LAG):
                if i < 16:
                    qk_step(i)
                if i >= LAG:
                    pv_step(i - LAG)
                yield

        def merge_slot(j):
            P.alias(ACCK, [("acc2", x) for x in range(8)])
            P.alias(DENK, [("den2", x) for x in range(8)])
            for q in range(4):
                sl = slice(q * 512, (q + 1) * 512)
                P.op("dve", lambda e, sl=sl: e.reciprocal(out=den[:, sl], in_=den[:, sl]), reads=[("den", q)], writes=[("den", q)])
                P.op("dve", lambda e, sl=sl: e.tensor_tensor(out=acc[:, sl], in0=acc[:, sl], in1=den[:, sl], op=ALU.mult),
                     reads=[("acc", q), ("den", q)], writes=[("acc", q)])
                P.op("pool", lambda e, sl=sl, j=j: e.tensor_tensor(out=og[:, j, sl], in0=acc[:, sl], in1=gs[:, sl], op=ALU.mult),
                     reads=[("acc", q), ("gs", q)], writes=[("og", j)])

        drain(proj_gen(0, 0))
        drain(proj_gen(0, 1))
        if b == 0:
            P.dma("pool", lambda e: e.dma_start(out=wao, in_=wao_d.rearrange("(k p) c -> p k c", p=128)), "wao", writes=["wao"])
            P.dma("pool", lambda e: e.dma_start(out=wco, in_=wco_d.rearrange("(k p) c -> p k c", p=128)), "wco", writes=["wco"])
        Wfc = None
        for j in range(4):
            interleave(attn_gen(j, 0), chain(proj_gen(j, 2), gproj_gen(j)), 2, lead=8)
            interleave(attn_gen(j, 1), proj_gen(j + 1, 0) if j < 3 else iter(()), 2)
            if j == 3:
                Wfc = [A.alloc(R_OFF + i * 2048, 2048, [("Wfc", i)], BF16).rearrange("p (k c) -> p k c", k=8) for i in range(2)]
                load_wfc(Wfc[0], 0)
                load_wfc(Wfc[1], 1)
            interleave(attn_gen(j, 2), proj_gen(j + 1, 1) if j < 3 else iter(()), 2)
            merge_slot(j)
        if stop == "s1":
            raise _Stop()

        ro = R_OFF + 5120
        sg = A.alloc(ro, 8192, [("sg", fc) for fc in range(8)], BF16).rearrange("p (k t) -> p k t", k=8); ro += 8192
        zb = []
        for i in range(2):
            zb.append(A.alloc(ro, 2056, [("z", i)])); ro += 2056
        tmp3 = []
        for i in range(2):
            d_ = {}
            for nm in ("u", "y", "sgl"):
                d_[nm] = A.alloc(ro, 512, [(nm, i)]); ro += 512
            tmp3.append(d_)
        assert ro <= ARENA_WORDS
        for i in range(2):
            P.op("pool", lambda e, i=i: e.memset(zb[i][:, 0:2], 0.0), writes=[("z", i)])
        it3 = 0
        Wm = None
        for fc in range(8):
            W = Wfc[fc % 2]
            z = zb[fc % 2]
            zk = ("z", fc % 2)
            for q in range(4):
                T = tmp3[it3 % 2]
                ti = it3 % 2
                it3 += 1
                banks = [(it3 % 2) * 4 + s_ for s_ in range(4)]
                for s_ in (0, 3, 2, 1):
                    for k in range(KC):
                        P.op("pe", lambda e, s_=s_, k=k, W=W, q=q, bank=banks[s_]: e.matmul(
                            ps[bank][:, :], lhsT=W[:, k, s_ * 128:(s_ + 1) * 128], rhs=hT[:, k, q * 512:(q + 1) * 512],
                            start=(k == 0), stop=(k == KC - 1)), reads=[("Wfc", fc % 2), ("hT", q, k)], writes=pk(banks[s_]))
                bu, bb, bc, bg = banks
                zs = slice(2 + q * 512, 2 + (q + 1) * 512)
                P.op("act", lambda e, T=T, bu=bu: e.activation(out=T["u"], in_=ps[bu][:, :], func=AF.Identity),
                     reads=pk(bu), writes=[("u", ti)])
                P.op("act", lambda e, T=T, bg=bg: e.activation(out=T["sgl"], in_=ps[bg][:, :], func=AF.Silu),
                     reads=pk(bg), writes=[("sgl", ti)])
                P.op("dve", lambda e, T=T, bc=bc, z=z, zs=zs: e.tensor_tensor(out=z[:, zs], in0=ps[bc][:, :], in1=T["u"], op=ALU.mult),
                     reads=pk(bc) + [("u", ti)], writes=[zk])
                P.op("dve", lambda e, T=T, bb=bb: e.tensor_tensor(out=T["sgl"], in0=ps[bb][:, :], in1=T["sgl"], op=ALU.mult),
                     reads=pk(bb) + [("sgl", ti)], writes=[("sgl", ti)])
                P.op("act", lambda e, T=T, z=z, zs=zs, fc=fc: e.activation(
                    out=T["y"], in_=z[:, zs], func=AF.Identity, scale=cwT[:, fc * 4 + 2:fc * 4 + 3], bias=cwT[:, fc * 4 + 3:fc * 4 + 4]),
                    reads=[zk, "cwT"], writes=[("y", ti)])
                P.op("dve", lambda e, T=T, z=z, q=q, fc=fc: e.scalar_tensor_tensor(
                    out=T["y"], in0=z[:, 1 + q * 512:1 + (q + 1) * 512], scalar=cwT[:, fc * 4 + 1:fc * 4 + 2], in1=T["y"],
                    op0=ALU.mult, op1=ALU.add), reads=[zk, "cwT", ("y", ti)], writes=[("y", ti)])
                P.op("dve", lambda e, T=T, z=z, q=q, fc=fc: e.scalar_tensor_tensor(
                    out=T["y"], in0=z[:, q * 512:(q + 1) * 512], scalar=cwT[:, fc * 4 + 0:fc * 4 + 1], in1=T["y"],
                    op0=ALU.mult, op1=ALU.add), reads=[zk, "cwT", ("y", ti)], writes=[("y", ti)])
                P.op("pool", lambda e, T=T, fc=fc, q=q: e.tensor_tensor(out=sg[:, fc, q * 512:(q + 1) * 512], in0=T["y"], in1=T["sgl"],
                                                                        op=ALU.mult),
                     reads=[("y", ti), ("sgl", ti)], writes=[("sg", fc)])
            if fc + 2 < 8:
                load_wfc(W, fc + 2)
            if fc == 6:
                Wm = [A.alloc(R_OFF + i * 1024, 1024, [("Wm", i)], BF16).rearrange("p (k c) -> p k c", k=8) for i in range(2)]
                load_wm(Wm[0], 0)
                load_wm(Wm[1], 1)
        if stop == "s3":
            raise _Stop()

        merged = A.alloc(R_OFF + 13312, 8192, [("mg", q) for q in range(4)], BF16).rearrange("p (k t) -> p k t", k=8)
        ro = R_OFF + 2048
        tmp4 = []
        for i in range(2):
            d_ = {}
            for nm in ("sa", "sc"):
                d_[nm] = A.alloc(ro, 512, [(nm, i)]); ro += 512
            tmp4.append(d_)
        assert ro <= R_OFF + 5120
        SGK = [("sg", fc) for fc in range(8)]
        OGK = [("og", j) for j in range(4)]
        it4 = 0
        W1n = None
        for dc in range(8):
            W = Wm[dc % 2]
            for q in range(4):
                T = tmp4[it4 % 2]
                ti = it4 % 2
                it4 += 1
                bA, bMA, bS, bMC = [(it4 % 2) * 4 + s_ for s_ in range(4)]
                qs = slice(q * 512, (q + 1) * 512)
                for k in range(KC):
                    P.op("pe", lambda e, k=k, W=W, qs=qs, bMA=bMA: e.matmul(
                        ps[bMA][:, :], lhsT=W[:, k, 0:128], rhs=hT[:, k, qs], start=(k == 0), stop=(k == KC - 1)),
                        reads=[("Wm", dc % 2), ("hT", q, k)], writes=pk(bMA))
                for k in range(KC):
                    P.op("pe", lambda e, k=k, W=W, qs=qs, bMC=bMC: e.matmul(
                        ps[bMC][:, :], lhsT=W[:, k, 128:256], rhs=hT[:, k, qs], start=(k == 0), stop=(k == KC - 1)),
                        reads=[("Wm", dc % 2), ("hT", q, k)], writes=pk(bMC))
                for k in range(4):
                    P.op("pe", lambda e, k=k, qs=qs, bA=bA, dc=dc: e.matmul(
                        ps[bA][:, :], lhsT=wao[:, k, dc * 128:(dc + 1) * 128], rhs=og[:, k, qs], start=(k == 0), stop=(k == 3)),
                        reads=["wao"] + OGK, writes=pk(bA))
                for k in range(KC):
                    P.op("pe", lambda e, k=k, qs=qs, bS=bS, dc=dc: e.matmul(
                        ps[bS][:, :], lhsT=wco[:, k, dc * 128:(dc + 1) * 128], rhs=sg[:, k, qs], start=(k == 0), stop=(k == KC - 1)),
                        reads=["wco"] + SGK, writes=pk(bS))
                P.op("act", lambda e, T=T, bMA=bMA: e.activation(out=T["sa"], in_=ps[bMA][:, :], func=AF.Sigmoid),
                     reads=pk(bMA), writes=[("sa", ti)])
                P.op("act", lambda e, T=T, bMC=bMC: e.activation(out=T["sc"], in_=ps[bMC][:, :], func=AF.Sigmoid),
                     reads=pk(bMC), writes=[("sc", ti)])
                P.op("dve", lambda e, T=T, bA=bA: e.tensor_tensor(out=T["sa"], in0=ps[bA][:, :], in1=T["sa"], op=ALU.mult),
                     reads=pk(bA) + [("sa", ti)], writes=[("sa", ti)])
                P.op("dve", lambda e, T=T, bS=bS: e.tensor_tensor(out=T["sc"], in0=ps[bS][:, :], in1=T["sc"], op=ALU.mult),
                     reads=pk(bS) + [("sc", ti)], writes=[("sc", ti)])
                P.op("pool", lambda e, T=T, dc=dc, qs=qs: e.tensor_tensor(out=merged[:, dc, qs], in0=T["sa"], in1=T["sc"], op=ALU.add),
                     reads=[("sa", ti), ("sc", ti)], writes=[("mg", q)])
            if dc + 2 < 8:
                load_wm(W, dc + 2)
        if stop == "s4":
            raise _Stop()
        s0n = None
        if has_next:
            W1n = alloc_w1()
            for g_ in range(3):
                load_wg(W1n, 0, g_)
            load_wgate(W1n, 0)
            s0n = stage0_gen(b + 1, False)

        ro = R_OFF + 5120
        xr, rb, obuf = [], [], []
        for i in range(3):
            xr.append(A.alloc(ro, 1024, [("xr", i)])); ro += 1024
        NRB = 5
        for i in range(NRB):
            rb.append(A.alloc(ro, 1024, [("rb", i)])); ro += 1024
        obuf = rb
        assert ro <= R_OFF + 13312
        ro = R_OFF + 21504
        stt = [A.alloc(ro + i * 16, 16, [("st", i)]) for i in range(NRB)]
        ro += 16 * NRB
        mvt = [A.alloc(ro + i * 8, 8, [("mv", i)]) for i in range(NRB)]
        ro += 8 * NRB
        assert ro <= ARENA_WORDS

        def xr_load(t):
            P.dma("sp", lambda e: e.dma_start(out=xr[t % 3], in_=x_d[b, t * 128:(t + 1) * 128, :]),
                  "xr%d" % (t % 3), writes=[("xr", t % 3)])
        xr_load(0)
        xr_load(1)
        def ph_A(t):
            i2, i3 = t % NRB, t % 3
            banks = [(t % 4) * 2, (t % 4) * 2 + 1]
            for half in range(2):
                for k in range(KC):
                    P.op("pe", lambda e, k=k, half=half, bank=banks[half]: e.matmul(
                        ps[bank][:, :], lhsT=merged[:, k, t * 128:(t + 1) * 128], rhs=wo[:, k, half * 512:(half + 1) * 512],
                        start=(k == 0), stop=(k == KC - 1)), reads=["wo", ("mg", t // 4)], writes=pk(banks[half]))
            for half in range(2):
                hs = slice(half * 512, (half + 1) * 512)
                P.op("dve", lambda e, hs=hs, bank=banks[half]: e.scalar_tensor_tensor(
                    out=rb[i2][:, hs], in0=xr[i3][:, hs], scalar=ALPHA, in1=ps[bank][:, :], op0=ALU.mult, op1=ALU.add),
                    reads=pk(banks[half]) + [("xr", i3)], writes=[("rb", i2)])
            for c_ in range(2):
                P.op("dve", lambda e, c_=c_: e.bn_stats(out=stt[i2][:, c_ * 6:(c_ + 1) * 6], in_=rb[i2][:, c_ * 512:(c_ + 1) * 512]),
                     reads=[("rb", i2)], writes=[("st", i2)])
            mv = mvt[i2]
            P.op("dve", lambda e: e.bn_aggr(out=mv[:, 0:2], in_=stt[i2][:, 0:12]), reads=[("st", i2)], writes=[("mv", i2)])
            P.op("act", lambda e: e.activation(out=mv[:, 2:3], in_=mv[:, 1:2], func=AF.Sqrt, bias=epst[:, 0:1], scale=1.0),
                 reads=[("mv", i2), "eps"], writes=[("mv", i2)])

        def ph_C(t):
            i2 = t % NRB
            mv = mvt[i2]
            P.op("dve", lambda e: e.reciprocal(out=mv[:, 3:4], in_=mv[:, 2:3]), reads=[("mv", i2)], writes=[("mv", i2)])
            P.op("dve", lambda e: e.scalar_tensor_tensor(out=mv[:, 4:5], in0=mv[:, 0:1], scalar=-1.0, in1=mv[:, 3:4],
                                                         op0=ALU.mult, op1=ALU.mult),
                 reads=[("mv", i2)], writes=[("mv", i2)])
            P.op("act", lambda e: e.activation(out=obuf[i2], in_=rb[i2], func=AF.Identity, scale=mv[:, 3:4], bias=mv[:, 4:5]),
                 reads=[("rb", i2), ("mv", i2)], writes=[("rb", i2)])

        def ph_E(t):
            i2 = t % NRB
            P.op("dve", lambda e: e.tensor_tensor(out=obuf[i2], in0=obuf[i2], in1=lng, op=ALU.mult),
                 reads=[("rb", i2), "lng"], writes=[("rb", i2)])
            P.op("pool", lambda e: e.tensor_tensor(out=obuf[i2], in0=obuf[i2], in1=lnb, op=ALU.add),
                 reads=[("rb", i2), "lnb"], writes=[("rb", i2)])
            P.dma("pool", lambda e: e.dma_start(out=y_d[b, t * 128:(t + 1) * 128, :], in_=obuf[i2]),
                  "yo%d" % i2, reads=[("rb", i2)])

        for t in range(16 + 2):
            if t < 16:
                if t + 2 < 16:
                    xr_load(t + 2)
                ph_A(t)
            if 0 <= t - 1 < 16:
                ph_C(t - 1)
            if 0 <= t - 2 < 16:
                ph_E(t - 2)
            if s0n is not None and t < 16:
                next(s0n, None)
        if s0n is not None:
            drain(s0n)
        return W1n

    W1c = alloc_w1()
    for g_ in range(3):
        load_wg(W1c, 0, g_)
    load_wgate(W1c, 0)
    try:
        if stop != "pro":
            drain(stage0_gen(0, True))
            late_consts()
            compute_gate1()
            if stop == "s0":
                raise _Stop()
            nseq = NSEQ if stop is None else 1
            for b_ in range(nseq):
                W1c = seq_body(b_, W1c, b_ + 1 < nseq)
    except _Stop:
        pass

    P.emit()
    return nc, dbg_outs


def prep_inputs(x, c, w_ada, b_ada, w_in, conv_w, conv_b, rel_bias, w_attn_out, w_conv_out, w_o, ln_g, ln_b):
    f = lambda a: np.ascontiguousarray(np.asarray(a, dtype=np.float32))
    x = f(x); c = f(c)
    perm = _w_in_perm()
    w_in_p = f(np.asarray(w_in)[0][:, perm])
    b_ada0 = np.asarray(b_ada, dtype=np.float32)[0]
    badaT = f(b_ada0[:2048].reshape(16, 128).T)
    bgate = f(np.broadcast_to(b_ada0[2048:][None, :], (128, D)))
    cw = np.asarray(conv_w, dtype=np.float32)[0]
    cbv = np.asarray(conv_b, dtype=np.float32)[0]
    cw4 = np.concatenate([cw, cbv[None, :]], axis=0)
    cwT = f(cw4.reshape(4, 8, 128).transpose(2, 1, 0).reshape(128, 32))
    idx, mask = _bias_index_and_mask()
    rb = np.asarray(rel_bias, dtype=np.float32)
    eb = np.empty((128, 12, 256), np.float32)
    for g in range(3):
        for j in range(4):
            h = g * 4 + j
            eb[:, h, :] = rb[:, h][idx[g]]
    shared = {
        "w_ada": f(np.asarray(w_ada)[0]), "badaT": badaT, "bgate": bgate, "w_in": w_in_p, "cwT": cwT,
        "ebsrc": f(eb.reshape(128, 12 * 256)), "mask": f(mask), "ident": np.eye(128, dtype=np.float32),
        "w_attn_out": f(np.asarray(w_attn_out)[0]), "w_conv_out": f(np.asarray(w_conv_out)[0]), "w_o": f(np.asarray(w_o)[0]),
        "lng": f(np.broadcast_to(np.asarray(ln_g, dtype=np.float32)[0][None, :], (128, D))),
        "lnb": f(np.broadcast_to(np.asarray(ln_b, dtype=np.float32)[0][None, :], (128, D))),
    }
    in_maps = []
    for i in range(N_CORES):
        m = dict(shared)
        m["x"] = f(x[i * NSEQ:(i + 1) * NSEQ])
        cc = c[i * NSEQ:(i + 1) * NSEQ]
        m["cT"] = f(cc.reshape(NSEQ, 8, 128).transpose(2, 1, 0).reshape(128, 16))
        in_maps.append(m)
    return in_maps


_NC_CACHE = {}


def kernel(x, c, w_ada, b_ada, w_in, conv_w, conv_b, rel_bias, w_attn_out, w_conv_out, w_o, ln_g, ln_b):
    in_maps = prep_inputs(x, c, w_ada, b_ada, w_in, conv_w, conv_b, rel_bias, w_attn_out, w_conv_out, w_o, ln_g, ln_b)
    if "nc" not in _NC_CACHE:
        _NC_CACHE["nc"] = build_program()[0]
    nc = _NC_CACHE["nc"]
    res = run_bass_kernel_spmd(nc, in_maps, core_ids=list(range(N_CORES)))
    out = np.concatenate([np.asarray(r["y"], dtype=np.float32) for r in res.results], axis=0)
    return out.reshape(N_CORES * NSEQ, S, D)
```
